# Optimizing a Trainium2 kernel written in Bass

```python
import math
import jax, jax.numpy as jnp
from jax import lax
import numpy as np

D_MODEL = 2048
BATCH = 1
SEQ = 8192
DEPTH = 4

GRID_W = 64
CTX_LEN = 256
HEAD_DIM = 128
NA_HEADS = 8
NA_WIDTH = NA_HEADS * HEAD_DIM
NA_KH_MAX = 8
NA_KW = 16
S5_WIDTH = 1024
S5_GROUP = 16
S5_GROUPS = S5_WIDTH // S5_GROUP
S5_STATE = 64
DT_MIN = 1e-3
DT_MAX = 1e-1
GQA_HEADS = 16
GQA_KV_HEADS = 4
GQA_WIDTH = GQA_HEADS * HEAD_DIM
GQA_KV_WIDTH = GQA_KV_HEADS * HEAD_DIM
ROPE_THETA = 10000.0
Q_BLOCK = 128
EVEN_IN = 4 * NA_WIDTH + 2 * S5_WIDTH
EVEN_OUT = NA_WIDTH + S5_WIDTH
ODD_IN = 2 * GQA_WIDTH + 2 * GQA_KV_WIDTH
DN_ALPHA = (2 * DEPTH) ** 0.25
DN_BETA = (8 * DEPTH) ** -0.25
LN_EPS = 1e-6
RMS_EPS = 1e-6

kernel_name = "hybrid_natten_s5_gqa_flow_backbone"


def layer_norm(x, g, b):
    xf = x.astype(jnp.float32)
    mu = jnp.mean(xf, -1, keepdims=True)
    var = jnp.mean(jnp.square(xf - mu), -1, keepdims=True)
    return ((xf - mu) * lax.rsqrt(var + LN_EPS) * g + b).astype(x.dtype)


def rms_norm(x, g):
    xf = x.astype(jnp.float32)
    return (xf * lax.rsqrt(jnp.mean(xf * xf, -1, keepdims=True) + RMS_EPS) * g).astype(x.dtype)


def ada_mod(cvec, w, b):
    m = jax.nn.silu(cvec) @ w + b
    return jnp.split(m, 3, -1)


def axial_rope(x):
    n = x.shape[1]
    t = jnp.arange(n)
    row = (t // GRID_W).astype(jnp.float32)
    col = (t % GRID_W).astype(jnp.float32)
    half = HEAD_DIM // 2
    per_axis = half // 2
    inv = ROPE_THETA ** (-jnp.arange(per_axis, dtype=jnp.float32) / per_axis)
    ang = jnp.concatenate([row[:, None] * inv, col[:, None] * inv], -1)
    cos = jnp.cos(ang)[None, :, None, :]
    sin = jnp.sin(ang)[None, :, None, :]
    xf = x.astype(jnp.float32)
    x1, x2 = xf[..., :half], xf[..., half:]
    return jnp.concatenate([x1 * cos - x2 * sin, x2 * cos + x1 * sin], -1).astype(x.dtype)


def gqa_attention(q, k, v):
    b, t, hq, dh = q.shape
    hkv = k.shape[2]
    qg = q.reshape(b, t, hkv, hq // hkv, dh)
    s = jnp.einsum('btkgd,bskd->bkgts', qg, k).astype(jnp.float32) * (dh ** -0.5)
    p = jax.nn.softmax(s, -1).astype(v.dtype)
    o = jnp.einsum('bkgts,bskd->btkgd', p, v)
    return o.reshape(b, t, hq, dh)


def neighbourhood_attention(q, k, v, qc, kc, vc, rpb, need_ctx):
    b, n, _ = q.shape
    rows = n // GRID_W
    kh = min(NA_KH_MAX, rows)
    scale = HEAD_DIM ** -0.5
    qg = q.reshape(b, rows, GRID_W, NA_HEADS, HEAD_DIM)
    kg = k.reshape(b, rows, GRID_W, NA_HEADS, HEAD_DIM)
    vg = v.reshape(b, rows, GRID_W, NA_HEADS, HEAD_DIM)
    kcx = kc.reshape(b, -1, NA_HEADS, HEAD_DIM)
    vcx = vc.reshape(b, -1, NA_HEADS, HEAD_DIM)
    r_start = jnp.clip(jnp.arange(rows) - kh // 2, 0, rows - kh)
    c_idx = jnp.arange(GRID_W)
    c_start = jnp.clip(c_idx - NA_KW // 2, 0, GRID_W - NA_KW)
    col_nb = c_start[:, None] + jnp.arange(NA_KW)
    dc = col_nb - c_idx[:, None] + NA_KW - 1
    n_loc = kh * NA_KW

    def one_row(r):
        rr = r_start[r] + jnp.arange(kh)
        k_nb = jnp.take(jnp.take(kg, rr, axis=1), col_nb, axis=2)
        v_nb = jnp.take(jnp.take(vg, rr, axis=1), col_nb, axis=2)
        dr = rr - r + NA_KH_MAX - 1
        bias = jnp.take(jnp.take(rpb, dr, axis=1), dc, axis=2)
        bias = bias.transpose(0, 2, 1, 3)
        qr = qg[:, r]
        s_loc = jnp.einsum('bwhd,bawkhd->bhwak', qr, k_nb).astype(jnp.float32) * scale + bias[None].astype(jnp.float32)
        s_ctx = jnp.einsum('bwhd,bchd->bhwc', qr, kcx).astype(jnp.float32) * scale
        s = jnp.concatenate([s_loc.reshape(b, NA_HEADS, GRID_W, n_loc), s_ctx], -1)
        p = jax.nn.softmax(s, -1).astype(v.dtype)
        p_loc = p[..., :n_loc].reshape(b, NA_HEADS, GRID_W, kh, NA_KW)
        p_ctx = p[..., n_loc:]
        return (jnp.einsum('bhwak,bawkhd->bwhd', p_loc, v_nb)
                + jnp.einsum('bhwc,bchd->bwhd', p_ctx, vcx))

    o = lax.map(one_row, jnp.arange(rows))
    o = o.transpose(1, 0, 2, 3, 4).reshape(b, n, NA_WIDTH)
    oc = None
    if need_ctx:
        qcx = qc.reshape(b, -1, NA_HEADS, HEAD_DIM)
        oc = gqa_attention(qcx, kcx, vcx).reshape(b, -1, NA_WIDTH)
    return o, oc


def s5_discretize(a_re, a_im, log_dt, b_re, b_im):
    a_re = a_re.astype(jnp.float32); a_im = a_im.astype(jnp.float32)
    b_re = b_re.astype(jnp.float32); b_im = b_im.astype(jnp.float32)
    dt = jnp.exp(log_dt.astype(jnp.float32))[..., None]
    mag = jnp.exp(a_re * dt)
    lb_re = mag * jnp.cos(a_im * dt)
    lb_im = mag * jnp.sin(a_im * dt)
    den = a_re * a_re + a_im * a_im
    n_re, n_im = lb_re - 1.0, lb_im
    f_re = (n_re * a_re + n_im * a_im) / den
    f_im = (n_im * a_re - n_re * a_im) / den
    bb_re = f_re[..., None] * b_re - f_im[..., None] * b_im
    bb_im = f_re[..., None] * b_im + f_im[..., None] * b_re
    return lb_re, lb_im, bb_re, bb_im


def complex_linear_scan(a_re, a_im, b_re, b_im, reverse):
    ar = jnp.broadcast_to(a_re, b_re.shape)
    ai = jnp.broadcast_to(a_im, b_re.shape)

    def combine(e1, e2):
        a1r, a1i, b1r, b1i = e1
        a2r, a2i, b2r, b2i = e2
        return (a1r * a2r - a1i * a2i, a1r * a2i + a1i * a2r,
                a2r * b1r - a2i * b1i + b2r, a2r * b1i + a2i * b1r + b2i)

    _, _, hr, hi = lax.associative_scan(combine, (ar, ai, b_re, b_im), reverse=reverse, axis=1)
    return hr, hi


def s5_bidirectional(u, uc, a_re, a_im, log_dt, b_re, b_im, c_re, c_im, d, need_ctx):
    lb_re, lb_im, bb_re, bb_im = s5_discretize(a_re, a_im, log_dt, b_re, b_im)
    c_re = c_re.astype(jnp.float32); c_im = c_im.astype(jnp.float32)
    df = d.astype(jnp.float32)
    uf = u.astype(jnp.float32)
    ucf = uc.astype(jnp.float32)

    def drive(z, k):
        zg = z.reshape(z.shape[0], z.shape[1], S5_GROUPS, S5_GROUP)
        return (jnp.einsum('blgc,gpc->blgp', zg, bb_re[k]),
                jnp.einsum('blgc,gpc->blgp', zg, bb_im[k]))

    def readout(hr, hi, k):
        y = jnp.einsum('blgp,gcp->blgc', hr, c_re[k]) - jnp.einsum('blgp,gcp->blgc', hi, c_im[k])
        return y.reshape(y.shape[0], y.shape[1], S5_WIDTH)

    y = uf * df
    yc = ucf * df
    for k, rev in ((0, False), (1, True)):
        brc, bic = drive(ucf, k)
        hrc, hic = complex_linear_scan(lb_re[k], lb_im[k], brc, bic, rev)
        end = 0 if rev else -1
        h0r, h0i = hrc[:, end], hic[:, end]
        br, bi = drive(uf, k)
        start = -1 if rev else 0
        br = br.at[:, start].add(lb_re[k] * h0r - lb_im[k] * h0i)
        bi = bi.at[:, start].add(lb_re[k] * h0i + lb_im[k] * h0r)
        hr, hi = complex_linear_scan(lb_re[k], lb_im[k], br, bi, rev)
        y = y + readout(hr, hi, k)
        if need_ctx:
            yc = yc + readout(hrc, hic, k)
    return y.astype(u.dtype), (yc.astype(uc.dtype) if need_ctx else None)


def even_layer(h, hc, w_in, w_out, rpb, a_re, a_im, log_dt, b_re, b_im, c_re, c_im, d, glu_w, glu_b, need_ctx):
    splits = [NA_WIDTH, 2 * NA_WIDTH, 3 * NA_WIDTH, 4 * NA_WIDTH, 4 * NA_WIDTH + S5_WIDTH]
    qa, ka, va, ga, ub, gb = jnp.split(h @ w_in, splits, -1)
    qac, kac, vac, gac, ubc, gbc = jnp.split(hc @ w_in, splits, -1)
    ya, yac = neighbourhood_attention(qa, ka, va, qac, kac, vac, rpb, need_ctx)
    yb, ybc = s5_bidirectional(ub, ubc, a_re, a_im, log_dt, b_re, b_im, c_re, c_im, d, need_ctx)

    def merge(za, zb, g_a, g_b):
        zb = jax.nn.gelu(zb)
        zb = zb * jax.nn.sigmoid(zb @ glu_w + glu_b)
        return jnp.concatenate([za * jax.nn.silu(g_a), zb * jax.nn.silu(g_b)], -1) @ w_out

    y = merge(ya, yb, ga, gb)
    yc = merge(yac, ybc, gac, gbc) if need_ctx else None
    return y, yc


def odd_layer(h, hc, w_in, w_out, q_g, k_g, need_ctx):
    b, n, _ = h.shape
    splits = [GQA_WIDTH, GQA_WIDTH + GQA_KV_WIDTH, GQA_WIDTH + 2 * GQA_KV_WIDTH]

    def project(z):
        bb, t, _ = z.shape
        q, k, v, g = jnp.split(z @ w_in, splits, -1)
        q = rms_norm(q.reshape(bb, t, GQA_HEADS, HEAD_DIM), q_g)
        k = rms_norm(k.reshape(bb, t, GQA_KV_HEADS, HEAD_DIM), k_g)
        v = v.reshape(bb, t, GQA_KV_HEADS, HEAD_DIM)
        return q, k, v, g

    q, k, v, g = project(h)
    qc, kc, vc, gc = project(hc)
    q = axial_rope(q)
    k = axial_rope(k)
    k_all = jnp.concatenate([k, kc], 1)
    v_all = jnp.concatenate([v, vc], 1)
    qb = q.reshape(b, n // Q_BLOCK, Q_BLOCK, GQA_HEADS, HEAD_DIM).transpose(1, 0, 2, 3, 4)
    o = lax.map(lambda qq: gqa_attention(qq, k_all, v_all), qb)
    o = o.transpose(1, 0, 2, 3, 4).reshape(b, n, GQA_WIDTH)
    y = (o * jax.nn.silu(g)) @ w_out
    yc = None
    if need_ctx:
        oc = gqa_attention(qc, kc, vc).reshape(b, hc.shape[1], GQA_WIDTH)
        yc = (oc * jax.nn.silu(gc)) @ w_out
    return y, yc


def setup_inputs(seed: int = 0) -> dict:
    key = jax.random.key(seed)
    ne = (DEPTH + 1) // 2
    no = DEPTH // 2
    f32 = jnp.float32

    def nrm(i, shape, s):
        return jax.random.normal(jax.random.fold_in(key, i), shape, f32) * s

    even_col = jnp.ones((EVEN_IN,), f32).at[2 * NA_WIDTH:3 * NA_WIDTH].set(DN_BETA)
    odd_col = jnp.ones((ODD_IN,), f32).at[GQA_WIDTH + GQA_KV_WIDTH:GQA_WIDTH + 2 * GQA_KV_WIDTH].set(DN_BETA)
    a_im0 = math.pi * jnp.arange(S5_STATE, dtype=f32)
    return {
        "x": nrm(0, (BATCH, SEQ, D_MODEL), 1.0),
        "c": nrm(1, (BATCH, D_MODEL), 1.0),
        "ctx": nrm(2, (BATCH, CTX_LEN, D_MODEL), 1.0),
        "c_ctx": nrm(3, (D_MODEL,), 1.0),
        "ada_w": nrm(4, (DEPTH, D_MODEL, 3 * D_MODEL), 0.5 * D_MODEL ** -0.5),
        "ada_b": nrm(5, (DEPTH, 3 * D_MODEL), 0.02),
        "ln_g": 1.0 + nrm(6, (DEPTH, D_MODEL), 0.02),
        "ln_b": nrm(7, (DEPTH, D_MODEL), 0.02),
        "ev_w_in": nrm(8, (ne, D_MODEL, EVEN_IN), D_MODEL ** -0.5) * even_col,
        "ev_w_out": nrm(9, (ne, EVEN_OUT, D_MODEL), DN_BETA * EVEN_OUT ** -0.5),
        "na_rpb": nrm(10, (ne, NA_HEADS, 2 * NA_KH_MAX - 1, 2 * NA_KW - 1), 0.02),
        "s5_a_re": -0.5 + nrm(11, (ne, 2, S5_GROUPS, S5_STATE), 0.01),
        "s5_a_im": a_im0 + nrm(12, (ne, 2, S5_GROUPS, S5_STATE), 0.01),
        "s5_log_dt": jax.random.uniform(jax.random.fold_in(key, 13), (ne, 2, S5_GROUPS), f32,
                                        minval=math.log(DT_MIN), maxval=math.log(DT_MAX)),
        "s5_b_re": nrm(14, (ne, 2, S5_GROUPS, S5_STATE, S5_GROUP), (2 * S5_GROUP) ** -0.5),
        "s5_b_im": nrm(15, (ne, 2, S5_GROUPS, S5_STATE, S5_GROUP), (2 * S5_GROUP) ** -0.5),
        "s5_c_re": nrm(16, (ne, 2, S5_GROUPS, S5_GROUP, S5_STATE), (2 * S5_STATE) ** -0.5),
        "s5_c_im": nrm(17, (ne, 2, S5_GROUPS, S5_GROUP, S5_STATE), (2 * S5_STATE) ** -0.5),
        "s5_d": nrm(18, (ne, S5_WIDTH), 1.0),
        "s5_glu_w": nrm(19, (ne, S5_WIDTH, S5_WIDTH), S5_WIDTH ** -0.5),
        "s5_glu_b": nrm(20, (ne, S5_WIDTH), 0.02),
        "od_w_in": nrm(21, (no, D_MODEL, ODD_IN), D_MODEL ** -0.5) * odd_col,
        "od_w_out": nrm(22, (no, GQA_WIDTH, D_MODEL), DN_BETA * GQA_WIDTH ** -0.5),
        "q_norm_g": 1.0 + nrm(23, (no, HEAD_DIM), 0.02),
        "k_norm_g": 1.0 + nrm(24, (no, HEAD_DIM), 0.02),
    }


def reference(x, c, ctx, c_ctx, ada_w, ada_b, ln_g, ln_b, ev_w_in, ev_w_out, na_rpb,
              s5_a_re, s5_a_im, s5_log_dt, s5_b_re, s5_b_im, s5_c_re, s5_c_im, s5_d,
              s5_glu_w, s5_glu_b, od_w_in, od_w_out, q_norm_g, k_norm_g):
    xc = ctx
    for l in range(DEPTH):
        need_ctx = l < DEPTH - 1
        sh, sc, gt = ada_mod(c, ada_w[l], ada_b[l])
        shc, scc, gtc = ada_mod(c_ctx, ada_w[l], ada_b[l])
        h = x * (1.0 + sc[:, None, :]) + sh[:, None, :]
        hc = xc * (1.0 + scc) + shc
        if l % 2 == 0:
            i = l // 2
            y, yc = even_layer(h, hc, ev_w_in[i], ev_w_out[i], na_rpb[i],
                               s5_a_re[i], s5_a_im[i], s5_log_dt[i], s5_b_re[i], s5_b_im[i],
                               s5_c_re[i], s5_c_im[i], s5_d[i], s5_glu_w[i], s5_glu_b[i], need_ctx)
        else:
            i = l // 2
            y, yc = odd_layer(h, hc, od_w_in[i], od_w_out[i], q_norm_g[i], k_norm_g[i], need_ctx)
        x = layer_norm(DN_ALPHA * x + gt[:, None, :] * y, ln_g[l], ln_b[l])
        if need_ctx:
            xc = layer_norm(DN_ALPHA * xc + gtc * yc, ln_g[l], ln_b[l])
    return x
```

```python
import math
from contextlib import ExitStack
import numpy as np
import ml_dtypes
import concourse.bass as bass
import concourse.mybir as mybir
from concourse.bass_utils import run_bass_kernel_spmd

F32 = mybir.dt.float32
BF16 = mybir.dt.bfloat16
AF = mybir.ActivationFunctionType
ALU = mybir.AluOpType

NCORE = 8
D = 2048
KT = 16
NLAT = 1024
NCTX = 256
NTOK = NLAT + NCTX
TB = [(0, 512), (512, 512), (1024, 256)]
SEQ = 8192
DN_ALPHA = 8.0 ** 0.25
LN_EPS = 1e-6
RMS_EPS = 1e-6
ENGS = ("pe", "act", "dve", "pool", "sp")
NDMA_SEMS = 24
TWO_PI = 2.0 * math.pi


class Prog:
    def __init__(self, nc):
        self.nc = nc
        self.ops = []
        self.last_writer = {}
        self.readers = {}
        self.n_dma = 0
        self.bar_deps = set()
        self.bar_applied = set(ENGS)
        self.bar_start = 0
        self.ps = [nc.alloc_psum_tensor(f"psb{i}", [128, 512], F32) for i in range(8)]
        self.rot = {}

    def bank(self, group, ids):
        i = self.rot.get(group, 0)
        self.rot[group] = i + 1
        b = ids[i % len(ids)]
        return self.ps[b], f"ps{b}"

    def _deps(self, eng, reads, writes):
        deps = set()
        for k in reads:
            w = self.last_writer.get(k)
            if w is not None:
                deps.add(w)
        for k in writes:
            w = self.last_writer.get(k)
            if w is not None:
                deps.add(w)
            for r in self.readers.get(k, ()):
                deps.add(r)
        if eng not in self.bar_applied:
            deps |= self.bar_deps
            self.bar_applied.add(eng)
        return deps

    def _commit(self, idx, reads, writes):
        for k in reads:
            self.readers.setdefault(k, []).append(idx)
        for k in writes:
            self.last_writer[k] = idx
            self.readers[k] = []

    def barrier(self):
        last = {}
        deps = set()
        for i in range(self.bar_start, len(self.ops)):
            o = self.ops[i]
            if o["dma"]:
                deps.add(i)
            else:
                last[o["eng"]] = i
        deps |= set(last.values())
        self.bar_deps = deps
        self.bar_applied = set()
        self.bar_start = len(self.ops)

    def op(self, eng, fn, reads=(), writes=()):
        idx = len(self.ops)
        deps = self._deps(eng, reads, writes)
        self.ops.append(dict(eng=eng, fn=fn, deps=deps, dma=False, signal=False))
        self._commit(idx, reads, writes)
        return idx

    def dma(self, eng, out, in_, reads=(), writes=(), **kw):
        idx = len(self.ops)
        deps = self._deps(eng, reads, writes)
        n = self.n_dma
        self.n_dma += 1
        self.ops.append(dict(eng=eng, fn=None, out=out, in_=in_, kw=kw, deps=deps,
                             dma=True, dma_n=n, signal=True))
        self._commit(idx, reads, writes)
        return idx

    def mm(self, out, lhsT, rhs, start, stop, reads, writes):
        return self.op("pe", lambda e: e.matmul(out, lhsT, rhs, start=start, stop=stop), reads, writes)

    def act(self, out, in_, func, reads, writes, scale=1.0, bias=0.0):
        return self.op("act", lambda e: e.activation(out, in_, func, bias=bias, scale=scale), reads, writes)

    def tt(self, eng, out, a, b, op, reads, writes):
        return self.op(eng, lambda e: e.tensor_tensor(out, a, b, op), reads, writes)

    def ts(self, eng, out, a, s1, s2, op0, op1, reads, writes):
        if s2 is None:
            return self.op(eng, lambda e: e.tensor_scalar(out, a, s1, None, op0), reads, writes)
        return self.op(eng, lambda e: e.tensor_scalar(out, a, s1, s2, op0, op1), reads, writes)

    def stt(self, eng, out, in0, scalar, in1, op0, op1, reads, writes):
        return self.op(eng, lambda e: e.scalar_tensor_tensor(out, in0, scalar, in1, op0, op1), reads, writes)

    def copy(self, eng, out, in_, reads, writes):
        return self.op(eng, lambda e: e.tensor_copy(out, in_), reads, writes)

    def memset(self, eng, ap, val, writes):
        return self.op(eng, lambda e: e.memset(ap, val), (), writes)

    def emit(self):
        nc = self.nc
        ops = self.ops

        def nosync(od, o, engname):
            return od["eng"] == engname and (not o["dma"]) and engname == "pe"

        for i, o in enumerate(ops):
            for d in o["deps"]:
                od = ops[d]
                if od["dma"]:
                    continue
                if nosync(od, o, o["eng"]):
                    continue
                od["signal"] = True
        cnt = {e: 0 for e in ENGS}
        for o in ops:
            if o["dma"]:
                n = o["dma_n"]
                o["sem_id"] = n % NDMA_SEMS
                o["target"] = 16 * (n // NDMA_SEMS + 1)
            elif o["signal"]:
                cnt[o["eng"]] += 1
                o["count"] = cnt[o["eng"]]
        esems = {e: nc.alloc_semaphore(f"s_{e}") for e in ENGS}
        dsems = [nc.alloc_semaphore(f"s_dma{j}") for j in range(NDMA_SEMS)]
        by_eng = {e: [] for e in ENGS}
        for i, o in enumerate(ops):
            by_eng[o["eng"]].append(i)

        def run(engname, engine):
            waited = {}
            for i in by_eng[engname]:
                o = ops[i]
                need = {}
                for d in o["deps"]:
                    od = ops[d]
                    if od["dma"]:
                        key = ("d", od["sem_id"])
                        val = od["target"]
                    else:
                        if nosync(od, o, engname):
                            continue
                        key = ("e", od["eng"])
                        val = od["count"]
                    if val > need.get(key, 0):
                        need[key] = val
                if o["dma"]:
                    prev = o["target"] - 16
                    if prev > 0:
                        key = ("d", o["sem_id"])
                        if prev > need.get(key, 0):
                            need[key] = prev
                for key, val in need.items():
                    if waited.get(key, 0) >= val:
                        continue
                    waited[key] = val
                    sem = dsems[key[1]] if key[0] == "d" else esems[key[1]]
                    engine.wait_ge(sem, val)
                if o["dma"]:
                    ins = engine.dma_start(out=o["out"], in_=o["in_"], **o["kw"])
                    ins.then_inc(dsems[o["sem_id"]], 16)
                else:
                    ins = o["fn"](engine)
                    if o["signal"]:
                        ins.then_inc(esems[engname], 1)

        with nc.Block() as block:
            @block.tensor
            def _(e):
                run("pe", e)

            @block.scalar
            def _(e):
                run("act", e)

            @block.vector
            def _(e):
                run("dve", e)

            @block.gpsimd
            def _(e):
                run("pool", e)

            @block.sync
            def _(e):
                run("sp", e)
                final = {}
                for o in ops:
                    if o["dma"]:
                        final[o["sem_id"]] = max(final.get(o["sem_id"], 0), o["target"])
                for sid, val in final.items():
                    e.wait_ge(dsems[sid], val)


def _din(nc, name, shape, dt=F32):
    return nc.dram_tensor(name, list(shape), dt, kind="ExternalInput").ap()


def _dout(nc, name, shape, dt=F32):
    return nc.dram_tensor(name, list(shape), dt, kind="ExternalOutput").ap()


def build_mod():
    nc = bass.Bass("TRN2", target_bir_lowering=False)
    cT = _din(nc, "cT", [128, 16, 2])
    aw = _din(nc, "aw", [4, 2048, 768])
    ab = _din(nc, "ab", [4, 768])
    out = _dout(nc, "mod", [4, 2, 768])
    P = Prog(nc)
    sb = lambda name, shape, dt: nc.alloc_sbuf_tensor("s_" + name, shape, dt)
    c_sb = sb("c_sb", [128, 16, 2], F32)
    s_sb = sb("s_sb", [128, 16, 2], F32)
    P.dma("sp", c_sb[:], cT, writes=["c"])
    P.act(s_sb[:], c_sb[:], AF.Silu, ["c"], ["s"])
    wbuf = [sb(f"w{i}", [128, 16, 768], F32) for i in range(2)]
    bias = [sb(f"b{i}", [2, 768], F32) for i in range(2)]
    osb = [sb(f"o{i}", [2, 768], F32) for i in range(2)]
    for l in range(4):
        q = l % 2
        for kk in range(4):
            P.dma("sp", wbuf[q][:, 4 * kk:4 * kk + 4, :],
                  aw[l, 512 * kk:512 * (kk + 1), :].rearrange("(kt p) c -> p kt c", p=128), writes=[f"w{q}_{kk}"])
        for r in range(2):
            P.dma("sp", bias[q][r:r + 1, :], ab[l:l + 1, :], writes=[f"b{q}"])
        for (c0, cn) in ((0, 512), (512, 256)):
            pt, pk = P.bank("m", [0, 1])
            for kt in range(16):
                P.mm(pt[0:2, 0:cn], s_sb[:, kt, :], wbuf[q][:, kt, c0:c0 + cn], kt == 0, kt == 15,
                     ["s", f"w{q}_{kt // 4}"], [pk])
            P.tt("dve", osb[q][:, c0:c0 + cn], pt[0:2, 0:cn], bias[q][:, c0:c0 + cn], ALU.add,
                 [pk, f"b{q}"], [f"o{q}"])
        P.dma("sp", out[l], osb[q][:], reads=[f"o{q}"])
    P.emit()
    return nc


def load_x_and_mod(P, nc, xT_d, modT_d):
    sb = lambda name, shape, dt: nc.alloc_sbuf_tensor("s_" + name, shape, dt)
    xT = sb("xT", [128, KT, NTOK], F32)
    modT = sb("modT", [128, 2, 3, KT], F32)
    P.dma("sp", modT[:], modT_d, writes=["modT"])
    for kk in range(4):
        P.dma("sp", xT[:, 4 * kk:4 * kk + 4, :], xT_d[:, 4 * kk:4 * kk + 4, :], writes=[f"xT{kk}"])
    return xT, modT


def make_hT(P, nc, xT, modT):
    sb = lambda name, shape, dt: nc.alloc_sbuf_tensor("s_" + name, shape, dt)
    sc1 = sb("sc1", [128, 2, KT], F32)
    P.ts("dve", sc1[:], modT[:, :, 1, :], 1.0, None, ALU.add, None, ["modT"], ["sc1"])
    hT = sb("hT", [128, KT, NTOK], BF16)
    for kt in range(KT):
        for s, (t0, n) in ((0, (0, NLAT)), (1, (NLAT, NCTX))):
            P.act(hT[:, kt, t0:t0 + n], xT[:, kt, t0:t0 + n], AF.Identity,
                  [f"xT{kt // 4}", "sc1", "modT"], [f"hT{kt}"],
                  scale=sc1[:, s, kt:kt + 1], bias=modT[:, s, 0, kt:kt + 1])
    return hT


def stream_weight_blocks(P, nc, w_d, nblk, name):
    bufs = [nc.alloc_sbuf_tensor(f"s_{name}{i}", [128, KT, 512], BF16) for i in range(2)]
    issued = {}

    def blk(i):
        if i not in issued:
            q = i % 2
            for kk in range(4):
                P.dma("pool", bufs[q][:, 4 * kk:4 * kk + 4, :],
                      w_d[512 * kk:512 * (kk + 1), 512 * i:512 * (i + 1)].rearrange("(kt p) c -> p kt c", p=128),
                      writes=[f"{name}{q}_{kk}"])
            issued[i] = True
        return bufs[i % 2], f"{name}{i % 2}"
    return blk


def lin_fm(P, wb, wkey, c, hT, hkey, t0, n, pt, pk):
    for kt in range(KT):
        P.mm(pt[:, 0:n], wb[:, kt, c * 128:(c + 1) * 128], hT[:, kt, t0:t0 + n], kt == 0, kt == KT - 1,
             [f"{wkey}_{kt // 4}", f"{hkey}{kt}"], [pk])


def outproj_ln(P, nc, st, zT, xT, modT, w_out_d, lnT_d, xo_d):
    sb = lambda name, shape, dt: st.enter_context(nc.sbuf_tensor("s_" + name, shape, dt))
    wo = sb("wo", [128, KT, D], BF16)
    for cb in range(4):
        for kk in range(4):
            P.dma("pool", wo[:, 4 * kk:4 * kk + 4, 512 * cb:512 * (cb + 1)],
                  w_out_d[512 * kk:512 * (kk + 1), 512 * cb:512 * (cb + 1)].rearrange("(kt p) c -> p kt c", p=128),
                  writes=[f"wo{cb}_{kk}"])
    lnT = sb("lnT", [128, 2, KT], F32)
    P.dma("sp", lnT[:], lnT_d, writes=["lnT"])
    ones32 = sb("ones32b", [128, 128], F32)
    P.memset("pool", ones32[:], 1.0, ["ones32b"])
    epsc = sb("epsc_ln", [128, 1], F32)
    P.memset("pool", epsc[:], LN_EPS, ["epsc_ln"])
    ax = [sb(f"ax{i}", [128, 512], F32) for i in range(2)]
    sq = [sb(f"sq{i}", [128, 512], F32) for i in range(2)]
    mean = sb("mean", [128, 512], F32)
    msq = sb("msq", [128, 512], F32)
    rstd = sb("rstd", [128, 512], F32)
    tmp = [sb(f"lt{i}", [128, 512], F32) for i in range(2)]
    for bi, (t0, n) in enumerate(TB):
        s = 0 if t0 < NLAT else 1
        s1t, s1k = P.bank("ln1", [4, 5])
        s2t, s2k = P.bank("ln2", [6, 7])
        for fb in range(KT):
            pt, pk = P.bank("op", [0, 1, 2, 3])
            for kt in range(KT):
                P.mm(pt[:, 0:n], wo[:, kt, fb * 128:(fb + 1) * 128], zT[:, kt, t0:t0 + n], kt == 0, kt == KT - 1,
                     [f"wo{fb // 4}_{kt // 4}", f"zT{kt}"], [pk])
            q = fb % 2
            xk = f"xT{fb // 4}"
            P.act(ax[q][:, 0:n], xT[:, fb, t0:t0 + n], AF.Identity, [xk], [f"ax{q}"], scale=DN_ALPHA)
            P.stt("dve", xT[:, fb, t0:t0 + n], pt[:, 0:n], modT[:, s, 2, fb:fb + 1], ax[q][:, 0:n],
                  ALU.mult, ALU.add, [pk, "modT", f"ax{q}"], [xk])
            P.act(sq[q][:, 0:n], xT[:, fb, t0:t0 + n], AF.Square, [xk], [f"sq{q}"])
            P.mm(s1t[:, 0:n], ones32[:], xT[:, fb, t0:t0 + n], fb == 0, fb == KT - 1, ["ones32b", xk], [s1k])
            P.mm(s2t[:, 0:n], ones32[:], sq[q][:, 0:n], fb == 0, fb == KT - 1, ["ones32b", f"sq{q}"], [s2k])
        P.ts("dve", mean[:, 0:n], s1t[:, 0:n], 1.0 / D, None, ALU.mult, None, [s1k], ["mean"])
        P.tt("dve", msq[:, 0:n], mean[:, 0:n], mean[:, 0:n], ALU.mult, ["mean"], ["msq"])
        P.stt("dve", rstd[:, 0:n], s2t[:, 0:n], 1.0 / D, msq[:, 0:n], ALU.mult, ALU.subtract, [s2k, "msq"], ["rstd"])
        P.act(rstd[:, 0:n], rstd[:, 0:n], AF.Sqrt, ["rstd", "epsc_ln"], ["rstd"], bias=epsc[:, 0:1])
        P.op("dve", (lambda o_, i_: (lambda e: e.reciprocal(o_, i_)))(rstd[:, 0:n], rstd[:, 0:n]), ["rstd"], ["rstd"])
        for fb in range(KT):
            q = fb % 2
            xk = f"xT{fb // 4}"
            eng = "dve" if fb % 2 == 0 else "pool"
            P.tt(eng, tmp[q][:, 0:n], xT[:, fb, t0:t0 + n], mean[:, 0:n], ALU.subtract, [xk, "mean"], [f"lt{q}"])
            P.tt(eng, tmp[q][:, 0:n], tmp[q][:, 0:n], rstd[:, 0:n], ALU.mult, [f"lt{q}", "rstd"], [f"lt{q}"])
            P.ts(eng, xT[:, fb, t0:t0 + n], tmp[q][:, 0:n], lnT[:, 0, fb:fb + 1], lnT[:, 1, fb:fb + 1],
                 ALU.mult, ALU.add, [f"lt{q}", "lnT"], [xk])
        for kk in range(4):
            P.dma("sp", xo_d[:, 4 * kk:4 * kk + 4, t0:t0 + n], xT[:, 4 * kk:4 * kk + 4, t0:t0 + n], reads=[f"xT{kk}"])


def build_a_odd():
    nc = bass.Bass("TRN2", target_bir_lowering=False)
    xT_d = _din(nc, "xT", [128, KT, NTOK])
    modT_d = _din(nc, "modT", [128, 2, 3, KT])
    w_d = _din(nc, "w_in", [D, 5120])
    cos_d = _din(nc, "cosT", [128, NTOK])
    sin_d = _din(nc, "sinT", [128, NTOK])
    g_d = _din(nc, "gcol", [128, 2])
    perm_d = _din(nc, "permT", [128, 128])
    QT_d = _dout(nc, "QT", [16, 128, NTOK], BF16)
    KT_d = _dout(nc, "KTo", [4, 128, NTOK], BF16)
    V_d = _dout(nc, "V", [NTOK, 512], BF16)
    GT_d = _dout(nc, "GT", [16, 128, NTOK], BF16)
    P = Prog(nc)
    sb = lambda name, shape, dt: nc.alloc_sbuf_tensor("s_" + name, shape, dt)
    xT, modT = load_x_and_mod(P, nc, xT_d, modT_d)
    hT = make_hT(P, nc, xT, modT)
    cosT = sb("cosT", [128, NTOK], F32)
    sinT = sb("sinT", [128, NTOK], F32)
    gcol = sb("gcol", [128, 2], F32)
    permT = sb("permT", [128, 128], F32)
    ones32 = sb("ones32", [128, 128], F32)
    P.dma("sp", cosT[:], cos_d, writes=["cosT"])
    P.dma("sp", sinT[:], sin_d, writes=["sinT"])
    P.dma("sp", gcol[:], g_d, writes=["gcol"])
    P.dma("sp", permT[:], perm_d, writes=["permT"])
    P.memset("pool", ones32[:], 1.0, ["ones32"])
    epsr = sb("epsr", [128, 1], F32)
    P.memset("pool", epsr[:], 128.0 * RMS_EPS, ["epsr"])
    P.ts("dve", gcol[:, 1:2], gcol[:, 1:2], math.sqrt(128.0), None, ALU.mult, None, ["gcol"], ["gcol"])
    blk = stream_weight_blocks(P, nc, w_d, 10, "wi")
    NB = 3
    sqb = [sb(f"sqb{i}", [128, 512], F32) for i in range(NB)]
    rsb = [sb(f"rsb{i}", [128, 512], F32) for i in range(NB)]
    qnb = [sb(f"qnb{i}", [128, 512], F32) for i in range(NB)]
    t1b = [sb(f"t1b{i}", [128, 512], F32) for i in range(NB)]
    t2b = [sb(f"t2b{i}", [128, 512], F32) for i in range(NB)]
    stg = [sb(f"stg{i}", [128, 512], BF16) for i in range(4)]
    vst = [sb(f"vst{i}", [128, 512], BF16) for i in range(2)]
    it = 0
    si = 0
    for cb in range(10):
        wb, wkey = blk(cb)
        if cb + 1 < 10:
            blk(cb + 1)
        if cb == 5:
            for tt in range(NTOK // 128):
                pt, pk = P.bank("main", [0, 1, 2, 3])
                for kt in range(KT):
                    P.mm(pt[:, :], hT[:, kt, tt * 128:(tt + 1) * 128], wb[:, kt, :], kt == 0, kt == KT - 1,
                         [f"{wkey}_{kt // 4}", f"hT{kt}"], [pk])
                q = tt % 2
                P.act(vst[q][:], pt[:, :], AF.Identity, [pk], [f"vst{q}"])
                P.dma("sp", V_d[tt * 128:(tt + 1) * 128, :], vst[q][:], reads=[f"vst{q}"])
            continue
        for c in range(4):
            for (t0, n) in TB:
                pt, pk = P.bank("main", [0, 1, 2, 3])
                lin_fm(P, wb, wkey, c, hT, "hT", t0, n, pt, pk)
                sq_ = si % 4
                si += 1
                if cb >= 6:
                    head = (cb - 6) * 4 + c
                    P.act(stg[sq_][:, 0:n], pt[:, 0:n], AF.Silu, [pk], [f"stg{sq_}"])
                    P.dma("sp", GT_d[head, :, t0:t0 + n], stg[sq_][:, 0:n], reads=[f"stg{sq_}"])
                    continue
                isq = cb < 4
                head = cb * 4 + c if isq else c
                gc = gcol[:, 0:1] if isq else gcol[:, 1:2]
                b = it % NB
                it += 1
                P.act(sqb[b][:, 0:n], pt[:, 0:n], AF.Square, [pk], [f"sqb{b}"])
                st_, sk = P.bank("ss", [4, 5])
                P.mm(st_[:, 0:n], ones32[:], sqb[b][:, 0:n], True, True, ["ones32", f"sqb{b}"], [sk])
                P.act(rsb[b][:, 0:n], st_[:, 0:n], AF.Sqrt, [sk, "epsr"], [f"rsb{b}"], bias=epsr[:, 0:1])
                P.op("dve", (lambda o_, i_: (lambda e: e.reciprocal(o_, i_)))(rsb[b][:, 0:n], rsb[b][:, 0:n]), [f"rsb{b}"], [f"rsb{b}"])
                P.stt("dve", qnb[b][:, 0:n], pt[:, 0:n], gc, rsb[b][:, 0:n], ALU.mult, ALU.mult,
                      [pk, "gcol", f"rsb{b}"], [f"qnb{b}"])
                wt, wk = P.bank("sw", [6, 7])
                P.mm(wt[:, 0:n], permT[:], qnb[b][:, 0:n], True, True, ["permT", f"qnb{b}"], [wk])
                P.tt("pool", t1b[b][:, 0:n], qnb[b][:, 0:n], cosT[:, t0:t0 + n], ALU.mult, [f"qnb{b}", "cosT"], [f"t1b{b}"])
                P.tt("dve", t2b[b][:, 0:n], wt[:, 0:n], sinT[:, t0:t0 + n], ALU.mult, [wk, "sinT"], [f"t2b{b}"])
                P.tt("pool", stg[sq_][:, 0:n], t1b[b][:, 0:n], t2b[b][:, 0:n], ALU.add, [f"t1b{b}", f"t2b{b}"], [f"stg{sq_}"])
                dst = QT_d if isq else KT_d
                P.dma("sp", dst[head, :, t0:t0 + n], stg[sq_][:, 0:n], reads=[f"stg{sq_}"])
    P.emit()
    return nc


def build_b_odd():
    nc = bass.Bass("TRN2", target_bir_lowering=False)
    NKT = SEQ // 128
    xT_d = _din(nc, "xT", [128, KT, NTOK])
    modT_d = _din(nc, "modT", [128, 2, 3, KT])
    QT_d = _din(nc, "QT", [16, 128, NTOK], BF16)
    GT_d = _din(nc, "GT", [16, 128, NTOK], BF16)
    KA_d = _din(nc, "KTall", [4, 128, SEQ], BF16)
    VA_d = _din(nc, "Vall", [4, 128, NKT, 128], BF16)
    KC_d = _din(nc, "KTc", [4, 128, NCTX], BF16)
    VC_d = _din(nc, "Vc", [4, 128, 2, 128], BF16)
    wo_d = _din(nc, "w_out", [D, D])
    lnT_d = _din(nc, "lnT", [128, 2, KT])
    xo_d = _dout(nc, "xo", [128, KT, NTOK])
    P = Prog(nc)
    sb0 = lambda name, shape, dt: nc.alloc_sbuf_tensor("s_" + name, shape, dt)
    xT, modT = load_x_and_mod(P, nc, xT_d, modT_d)
    zT = sb0("zT", [128, KT, NTOK], BF16)
    with ExitStack() as st:
        sb = lambda name, shape, dt: st.enter_context(nc.sbuf_tensor("s_" + name, shape, dt))
        Ksb = sb("Ksb", [128, SEQ + NCTX], BF16)
        Vsb = sb("Vsb", [128, NKT + 2, 128], BF16)
        Qsb = [sb(f"Qsb{i}", [128, NTOK], BF16) for i in range(2)]
        Gsb = [sb(f"Gsb{i}", [128, NTOK], BF16) for i in range(2)]
        PT = [sb(f"PT{i}", [128, 512], BF16) for i in range(3)]
        rec = [sb(f"rec{i}", [128, 512], F32) for i in range(2)]
        z1 = [sb(f"z1{i}", [128, 512], F32) for i in range(2)]
        onesb = sb("onesb", [128, 128], BF16)
        P.memset("pool", onesb[:], 1.0, ["onesb"])
        pi = 0
        fi = 0
        for kv in range(4):
            for kk in range(4):
                P.dma("sp", Ksb[:, 2048 * kk:2048 * (kk + 1)], KA_d[kv, :, 2048 * kk:2048 * (kk + 1)], writes=[f"K{kk}"])
                P.dma("sp", Vsb[:, 16 * kk:16 * (kk + 1), :], VA_d[kv, :, 16 * kk:16 * (kk + 1), :], writes=[f"V{kk}"])
            P.dma("sp", Ksb[:, SEQ:SEQ + NCTX], KC_d[kv], writes=["K4"])
            P.dma("sp", Vsb[:, NKT:NKT + 2, :], VC_d[kv], writes=["V4"])
            for hh in range(4):
                h = kv * 4 + hh
                qb = h % 2
                P.dma("sp", Qsb[qb][:], QT_d[h], writes=[f"Q{qb}"])
                P.dma("sp", Gsb[qb][:], GT_d[h], writes=[f"G{qb}"])
                for (t0, n) in TB:
                    kts = list(range(NKT + 2)) if t0 < NLAT else [NKT, NKT + 1]
                    ot, ok = P.bank("O", [4, 5])
                    dt_, dk = P.bank("Dn", [6, 7])
                    for j, kt in enumerate(kts):
                        s_t, s_k = P.bank("S", [0, 1, 2, 3])
                        P.mm(s_t[:, 0:n], Ksb[:, kt * 128:(kt + 1) * 128], Qsb[qb][:, t0:t0 + n], True, True,
                             [f"K{min(kt // 16, 4)}", f"Q{qb}"], [s_k])
                        p = pi % 3
                        pi += 1
                        P.act(PT[p][:, 0:n], s_t[:, 0:n], AF.Exp, [s_k], [f"PT{p}"])
                        P.mm(ot[:, 0:n], Vsb[:, kt, :], PT[p][:, 0:n], j == 0, j == len(kts) - 1,
                             [f"V{min(kt // 16, 4)}", f"PT{p}"], [ok])
                        P.mm(dt_[:, 0:n], onesb[:], PT[p][:, 0:n], j == 0, j == len(kts) - 1,
                             ["onesb", f"PT{p}"], [dk])
                    f = fi % 2
                    fi += 1
                    P.op("dve", (lambda o_, i_: (lambda e: e.reciprocal(o_, i_)))(rec[f][:, 0:n], dt_[:, 0:n]),
                         [dk], [f"rec{f}"])
                    P.tt("dve", z1[f][:, 0:n], ot[:, 0:n], rec[f][:, 0:n], ALU.mult, [ok, f"rec{f}"], [f"z1{f}"])
                    P.tt("pool", zT[:, h, t0:t0 + n], z1[f][:, 0:n], Gsb[qb][:, t0:t0 + n], ALU.mult,
                         [f"z1{f}", f"G{qb}"], [f"zT{h}"])
        P.barrier()
    with ExitStack() as st:
        outproj_ln(P, nc, st, zT, xT, modT, wo_d, lnT_d, xo_d)
    P.emit()
    return nc


_PROGS = {}
BF = ml_dtypes.bfloat16


def _run(name, builder, in_maps):
    if name not in _PROGS:
        _PROGS[name] = builder()
    res = run_bass_kernel_spmd(_PROGS[name], in_maps, core_ids=list(range(NCORE)))
    return res.results


def to_fm(X):
    T = X.shape[0]
    return np.ascontiguousarray(X.reshape(T, KT, 128).transpose(2, 1, 0))


def from_fm(xT):
    T = xT.shape[2]
    return np.ascontiguousarray(xT.transpose(2, 1, 0).reshape(T, D))


def host_mod(c, c_ctx, ada_w, ada_b):
    cv = np.stack([c.reshape(D), c_ctx.reshape(D)], 1)
    cT = np.ascontiguousarray(cv.reshape(KT, 128, 2).transpose(1, 0, 2))
    in_maps = []
    for j in range(NCORE):
        in_maps.append({"cT": cT,
                        "aw": np.ascontiguousarray(ada_w[:, :, 768 * j:768 * (j + 1)]),
                        "ab": np.ascontiguousarray(ada_b[:, 768 * j:768 * (j + 1)])})
    res = _run("mod", build_mod, in_maps)
    mod = np.concatenate([r["mod"] for r in res], axis=2)
    return mod


def mod_to_T(mod_l):
    return np.ascontiguousarray(mod_l.reshape(2, 3, KT, 128).transpose(3, 0, 1, 2))


def ln_to_T(g, b):
    return np.ascontiguousarray(np.stack([g, b], 0).reshape(2, KT, 128).transpose(2, 0, 1))


def rope_tables():
    t = np.arange(SEQ)
    row = (t // 64).astype(np.float32)
    col = (t % 64).astype(np.float32)
    inv = (np.float32(10000.0) ** (-np.arange(32, dtype=np.float32) / np.float32(32))).astype(np.float32)
    ang = np.concatenate([row[:, None] * inv, col[:, None] * inv], -1).astype(np.float32)
    cos = np.cos(ang).astype(np.float32)
    sin = np.sin(ang).astype(np.float32)
    cosT = np.concatenate([cos, cos], 1).T
    sinT = np.concatenate([-sin, sin], 1).T
    outs = []
    for j in range(NCORE):
        c = np.ones((128, NTOK), np.float32)
        s = np.zeros((128, NTOK), np.float32)
        c[:, :NLAT] = cosT[:, NLAT * j:NLAT * (j + 1)]
        s[:, :NLAT] = sinT[:, NLAT * j:NLAT * (j + 1)]
        outs.append((np.ascontiguousarray(c), np.ascontiguousarray(s)))
    return outs


def run_odd_layer(xTs, modT, w_in, w_out, qg, kg, lnT):
    tabs = rope_tables()
    gcol = np.ascontiguousarray(np.stack([qg, kg], 1).astype(np.float32))
    permT = np.zeros((128, 128), np.float32)
    for m in range(128):
        permT[(m + 64) % 128, m] = 1.0
    in_maps = [{"xT": xTs[j], "modT": modT, "w_in": w_in, "cosT": tabs[j][0], "sinT": tabs[j][1],
                "gcol": gcol, "permT": permT} for j in range(NCORE)]
    ra = _run("a_odd", build_a_odd, in_maps)
    KTall = np.ascontiguousarray(np.concatenate([r["KTo"][:, :, :NLAT] for r in ra], axis=2))
    KTc = np.ascontiguousarray(ra[0]["KTo"][:, :, NLAT:])
    Vlat = np.concatenate([r["V"][:NLAT] for r in ra], axis=0)
    Vall = np.ascontiguousarray(Vlat.reshape(SEQ // 128, 128, 4, 128).transpose(2, 1, 0, 3))
    Vc = np.ascontiguousarray(ra[0]["V"][NLAT:].reshape(2, 128, 4, 128).transpose(2, 1, 0, 3))
    in_maps = [{"xT": xTs[j], "modT": modT, "QT": ra[j]["QT"], "GT": ra[j]["GT"], "KTall": KTall, "Vall": Vall,
                "KTc": KTc, "Vc": Vc, "w_out": w_out, "lnT": lnT} for j in range(NCORE)]
    rb = _run("b_odd", build_b_odd, in_maps)
    return [r["xo"] for r in rb], ra


MAGIC = 12582912.0
SIN_SCALE = TWO_PI - 1e-6


def rev_ap(t, pstep, off, n):
    return bass.AP(t, off + n - 1, [[pstep, 128], [-1, n]])


class S5:
    pass


def s5_prep(P, nc, sb, s5p_d, iota_d, need_F):
    S = S5()
    prm = sb("s5prm", [128, 3, 64], F32)
    P.dma("sp", prm[:], s5p_d, writes=["s5prm"])
    S.iota = sb("s5iota", [128, 1024], F32)
    P.dma("sp", S.iota[:], iota_d, writes=["s5iota"])
    dtc = sb("s5dt", [128, 64], F32)
    lnr = sb("s5lnr", [128, 64], F32)
    S.r = sb("s5r", [128, 64], F32)
    S.th2 = sb("s5th2", [128, 64], F32)
    P.act(dtc[:], prm[:, 2, :], AF.Exp, ["s5prm"], ["s5dt"])
    P.tt("dve", lnr[:], prm[:, 0, :], dtc[:], ALU.mult, ["s5prm", "s5dt"], ["s5lnr"])
    P.act(S.r[:], lnr[:], AF.Exp, ["s5lnr"], ["s5r"])
    P.tt("dve", S.th2[:], prm[:, 1, :], dtc[:], ALU.mult, ["s5prm", "s5dt"], ["s5th2"])
    P.ts("dve", S.th2[:], S.th2[:], 1.0 / TWO_PI, None, ALU.mult, None, ["s5th2"], ["s5th2"])
    ta = sb("s5ta", [128, 64], F32)
    tk = sb("s5tk", [128, 64], F32)

    def sincos(y_ap, ykey, mult, name):
        c = sb(name + "c", [128, 64], F32)
        s = sb(name + "s", [128, 64], F32)
        for off, dst, dk in ((0.0, s, name + "s"), (0.25, c, name + "c")):
            P.ts("dve", ta[:], y_ap, float(mult), off, ALU.mult, ALU.add, [ykey], ["s5ta"])
            P.ts("dve", tk[:], ta[:], MAGIC, MAGIC, ALU.add, ALU.subtract, ["s5ta"], ["s5tk"])
            P.tt("dve", ta[:], ta[:], tk[:], ALU.subtract, ["s5ta", "s5tk"], ["s5ta"])
            P.act(dst[:], ta[:], AF.Sin, ["s5ta"], [dk], scale=SIN_SCALE)
        return c, s
    S.c1, S.s1 = sincos(S.th2[:], "s5th2", 1.0, "s5o")
    if need_F:
        a_re = prm[:, 0, :]
        a_im = prm[:, 1, :]
        lbr = sb("s5lbr", [128, 64], F32)
        lbi = sb("s5lbi", [128, 64], F32)
        den = sb("s5den", [128, 64], F32)
        t1 = sb("s5t1", [128, 64], F32)
        S.Fre = sb("s5Fre", [128, 64], F32)
        S.Fim = sb("s5Fim", [128, 64], F32)
        S.nFre = sb("s5nFre", [128, 64], F32)
        P.tt("dve", lbr[:], S.r[:], S.c1[:], ALU.mult, ["s5r", "s5oc"], ["s5lbr"])
        P.ts("dve", lbr[:], lbr[:], -1.0, None, ALU.add, None, ["s5lbr"], ["s5lbr"])
        P.tt("dve", lbi[:], S.r[:], S.s1[:], ALU.mult, ["s5r", "s5os"], ["s5lbi"])
        P.tt("dve", den[:], a_re, a_re, ALU.mult, ["s5prm"], ["s5den"])
        P.tt("dve", t1[:], a_im, a_im, ALU.mult, ["s5prm"], ["s5t1"])
        P.tt("dve", den[:], den[:], t1[:], ALU.add, ["s5den", "s5t1"], ["s5den"])
        P.op("dve", lambda e: e.reciprocal(den[:], den[:]), ["s5den"], ["s5den"])
        P.tt("dve", S.Fre[:], lbr[:], a_re, ALU.mult, ["s5lbr", "s5prm"], ["s5Fre"])
        P.tt("dve", t1[:], lbi[:], a_im, ALU.mult, ["s5lbi", "s5prm"], ["s5t1"])
        P.tt("dve", S.Fre[:], S.Fre[:], t1[:], ALU.add, ["s5Fre", "s5t1"], ["s5Fre"])
        P.tt("dve", S.Fre[:], S.Fre[:], den[:], ALU.mult, ["s5Fre", "s5den"], ["s5Fre"])
        P.tt("dve", S.Fim[:], lbi[:], a_re, ALU.mult, ["s5lbi", "s5prm"], ["s5Fim"])
        P.tt("dve", t1[:], lbr[:], a_im, ALU.mult, ["s5lbr", "s5prm"], ["s5t1"])
        P.tt("dve", S.Fim[:], S.Fim[:], t1[:], ALU.subtract, ["s5Fim", "s5t1"], ["s5Fim"])
        P.tt("dve", S.Fim[:], S.Fim[:], den[:], ALU.mult, ["s5Fim", "s5den"], ["s5Fim"])
        P.ts("dve", S.nFre[:], S.Fre[:], -1.0, None, ALU.mult, None, ["s5Fre"], ["s5nFre"])
        rT = sb("s5rT", [128, 64], F32)
        P.act(rT[:], lnr[:], AF.Exp, ["s5lnr"], ["s5rT"], scale=float(NLAT))
        cT_, sT_ = sincos(S.th2[:], "s5th2", float(NLAT), "s5T")
        S.Are = sb("s5Are", [128, 64], F32)
        S.Aim = sb("s5Aim", [128, 64], F32)
        P.tt("dve", S.Are[:], rT[:], cT_[:], ALU.mult, ["s5rT", "s5Tc"], ["s5Are"])
        P.tt("dve", S.Aim[:], rT[:], sT_[:], ALU.mult, ["s5rT", "s5Ts"], ["s5Aim"])
    return S


def s5_alloc_work(S, sb):
    S.ty = sb("s5ty", [128, 1024], F32)
    S.tk2 = sb("s5tk2", [128, 1024], F32)
    S.cs = sb("s5cs", [128, 1024], F32)
    S.sn = sb("s5sn", [128, 1024], F32)
    S.m = [sb(f"s5m{i}", [128, 512], F32) for i in range(4)]
    S.gin = [sb(f"s5gin{i}", [128, 1024], F32) for i in range(2)]
    S.g = [sb(f"s5g{i}", [128, 1024], F32) for i in range(2)]
    S.ecol = sb("s5ecol", [128, 4], F32)


def s5_tables(P, S, kp, T):
    th = S.th2[:, kp:kp + 1]
    for off, dst, dk in ((0.0, S.sn, "s5sn"), (0.25, S.cs, "s5cs")):
        P.ts("pool", S.ty[:, 0:T], S.iota[:, 0:T], th, off, ALU.mult, ALU.add, ["s5iota", "s5th2"], ["s5ty"])
        P.ts("pool", S.tk2[:, 0:T], S.ty[:, 0:T], MAGIC, MAGIC, ALU.add, ALU.subtract, ["s5ty"], ["s5tk2"])
        P.tt("pool", S.ty[:, 0:T], S.ty[:, 0:T], S.tk2[:, 0:T], ALU.subtract, ["s5ty", "s5tk2"], ["s5ty"])
        P.act(dst[:, 0:T], S.ty[:, 0:T], AF.Sin, ["s5ty"], [dk], scale=SIN_SCALE)


def s5_drive_scan(P, S, kp, T, Bre_ap, Bim_ap, bkeys, u_ap_fn, ukeys, init):
    blocks = [(0, 512), (512, 512)] if T == 1024 else [(0, T)]
    for (t0, n) in blocks:
        xr, xrk = P.bank("s5x", [4, 5, 6, 7])
        xi, xik = P.bank("s5x", [4, 5, 6, 7])
        P.mm(xr[:, 0:n], Bre_ap, u_ap_fn(t0, n), True, True, bkeys + ukeys, [xrk])
        P.mm(xi[:, 0:n], Bim_ap, u_ap_fn(t0, n), True, True, bkeys + ukeys, [xik])
        m = S.m
        P.tt("dve", m[0][:, 0:n], xr[:, 0:n], S.cs[:, t0:t0 + n], ALU.mult, [xrk, "s5cs"], ["s5m0"])
        P.tt("dve", m[1][:, 0:n], xi[:, 0:n], S.sn[:, t0:t0 + n], ALU.mult, [xik, "s5sn"], ["s5m1"])
        P.tt("dve", m[2][:, 0:n], xi[:, 0:n], S.cs[:, t0:t0 + n], ALU.mult, [xik, "s5cs"], ["s5m2"])
        P.tt("dve", m[3][:, 0:n], xr[:, 0:n], S.sn[:, t0:t0 + n], ALU.mult, [xrk, "s5sn"], ["s5m3"])
        P.tt("pool", S.gin[0][:, t0:t0 + n], m[0][:, 0:n], m[1][:, 0:n], ALU.add, ["s5m0", "s5m1"], ["s5gin0"])
        P.tt("pool", S.gin[1][:, t0:t0 + n], m[2][:, 0:n], m[3][:, 0:n], ALU.subtract, ["s5m2", "s5m3"], ["s5gin1"])
    rb = bass.AP(S.r, kp, [[64, 128], [0, T]])
    for c in range(2):
        if init is None:
            ini, ik = 0.0, []
        else:
            ini, ik = init[c], init[2]
        P.op("dve", (lambda o_, d1_, ini_: (lambda e: e.tensor_tensor_scan(o_, rb, d1_, ini_, ALU.mult, ALU.add)))(
            S.g[c][:, 0:T], S.gin[c][:, 0:T], ini), ["s5r", f"s5gin{c}"] + ik, [f"s5g{c}"])


def s5_end_state(P, S, T, ere_ap, eim_ap, ekey):
    e = S.ecol
    gr = S.g[0][:, T - 1:T]
    gi = S.g[1][:, T - 1:T]
    c = S.cs[:, T - 1:T]
    s = S.sn[:, T - 1:T]
    P.tt("dve", e[:, 0:1], gr, c, ALU.mult, ["s5g0", "s5cs"], ["s5ecol"])
    P.tt("dve", e[:, 1:2], gi, s, ALU.mult, ["s5g1", "s5sn"], ["s5ecol"])
    P.tt("dve", e[:, 2:3], gr, s, ALU.mult, ["s5g0", "s5sn"], ["s5ecol"])
    P.tt("dve", e[:, 3:4], gi, c, ALU.mult, ["s5g1", "s5cs"], ["s5ecol"])
    P.tt("dve", ere_ap, e[:, 0:1], e[:, 1:2], ALU.subtract, ["s5ecol"], [ekey])
    P.tt("dve", eim_ap, e[:, 2:3], e[:, 3:4], ALU.add, ["s5ecol"], [ekey])


def build_a_even():
    nc = bass.Bass("TRN2", target_bir_lowering=False)
    xT_d = _din(nc, "xT", [128, KT, NTOK])
    modT_d = _din(nc, "modT", [128, 2, 3, KT])
    w_d = _din(nc, "w_in", [D, 6144])
    s5p_d = _din(nc, "s5p", [128, 3, 64])
    iota_d = _din(nc, "iota", [128, 1024])
    Bre_d = _din(nc, "Bre", [8, 128, 8, 128])
    Bim_d = _din(nc, "Bim", [8, 128, 8, 128])
    QT_d = _dout(nc, "QT", [8, 128, NTOK], BF16)
    KT_d = _dout(nc, "KTo", [8, 128, NTOK], BF16)
    V_d = _dout(nc, "V", [NTOK, 1024], BF16)
    GA_d = _dout(nc, "GAT", [8, 128, NTOK], BF16)
    UT_d = _dout(nc, "UT", [8, 128, NTOK], F32)
    GB_d = _dout(nc, "GBT", [8, 128, NTOK], BF16)
    E_d = _dout(nc, "E", [128, 2, 64], F32)
    P = Prog(nc)
    sb0 = lambda name, shape, dt: nc.alloc_sbuf_tensor("s_" + name, shape, dt)
    uTb = sb0("uTb", [128, 8, NLAT], BF16)
    uTr = sb0("uTr", [128, 8, NLAT], BF16)
    with ExitStack() as st:
        sb = lambda name, shape, dt: st.enter_context(nc.sbuf_tensor("s_" + name, shape, dt))
        xT = sb("xT", [128, KT, NTOK], F32)
        modT = sb("modT", [128, 2, 3, KT], F32)
        P.dma("sp", modT[:], modT_d, writes=["modT"])
        for kk in range(4):
            P.dma("sp", xT[:, 4 * kk:4 * kk + 4, :], xT_d[:, 4 * kk:4 * kk + 4, :], writes=[f"xT{kk}"])
        sc1 = sb("sc1", [128, 2, KT], F32)
        P.ts("dve", sc1[:], modT[:, :, 1, :], 1.0, None, ALU.add, None, ["modT"], ["sc1"])
        hT = sb("hT", [128, KT, NTOK], BF16)
        for kt in range(KT):
            for s, (t0, n) in ((0, (0, NLAT)), (1, (NLAT, NCTX))):
                P.act(hT[:, kt, t0:t0 + n], xT[:, kt, t0:t0 + n], AF.Identity,
                      [f"xT{kt // 4}", "sc1", "modT"], [f"hT{kt}"],
                      scale=sc1[:, s, kt:kt + 1], bias=modT[:, s, 0, kt:kt + 1])
        bufs = [sb(f"wi{i}", [128, KT, 512], BF16) for i in range(2)]
        issued = {}

        def blk(i):
            if i not in issued:
                q = i % 2
                for kk in range(4):
                    P.dma("pool", bufs[q][:, 4 * kk:4 * kk + 4, :],
                          w_d[512 * kk:512 * (kk + 1), 512 * i:512 * (i + 1)].rearrange("(kt p) c -> p kt c", p=128),
                          writes=[f"wi{q}_{kk}"])
                issued[i] = True
            return bufs[i % 2], f"wi{i % 2}"
        stg = [sb(f"stg{i}", [128, 512], BF16) for i in range(4)]
        stf = [sb(f"stf{i}", [128, 512], F32) for i in range(2)]
        vst = [sb(f"vst{i}", [128, 512], BF16) for i in range(2)]
        si = 0
        fi = 0
        for cb in range(12):
            wb, wkey = blk(cb)
            if cb + 1 < 12:
                blk(cb + 1)
            if cb in (4, 5):
                for tt in range(NTOK // 128):
                    pt, pk = P.bank("main", [0, 1, 2, 3])
                    for kt in range(KT):
                        P.mm(pt[:, :], hT[:, kt, tt * 128:(tt + 1) * 128], wb[:, kt, :], kt == 0, kt == KT - 1,
                             [f"{wkey}_{kt // 4}", f"hT{kt}"], [pk])
                    q = tt % 2
                    P.act(vst[q][:], pt[:, :], AF.Identity, [pk], [f"vst{q}"])
                    P.dma("sp", V_d[tt * 128:(tt + 1) * 128, 512 * (cb - 4):512 * (cb - 3)], vst[q][:], reads=[f"vst{q}"])
                continue
            for c in range(4):
                idx = (cb % 2) * 4 + c
                for (t0, n) in TB:
                    pt, pk = P.bank("main", [0, 1, 2, 3])
                    lin_fm(P, wb, wkey, c, hT, "hT", t0, n, pt, pk)
                    if cb in (8, 9):
                        f = fi % 2
                        fi += 1
                        P.act(stf[f][:, 0:n], pt[:, 0:n], AF.Identity, [pk], [f"stf{f}"])
                        P.dma("sp", UT_d[idx, :, t0:t0 + n], stf[f][:, 0:n], reads=[f"stf{f}"])
                        if t0 < NLAT:
                            P.copy("dve", uTb[:, idx, t0:t0 + n], stf[f][:, 0:n], [f"stf{f}"], [f"uTb{idx}"])
                            P.copy("dve", uTr[:, idx, NLAT - t0 - n:NLAT - t0],
                                   rev_ap(stf[f], 512, 0, n), [f"stf{f}"], [f"uTr{idx}"])
                        continue
                    sq_ = si % 4
                    si += 1
                    if cb in (0, 1):
                        P.act(stg[sq_][:, 0:n], pt[:, 0:n], AF.Identity, [pk], [f"stg{sq_}"], scale=128.0 ** -0.5)
                        dst = QT_d
                    elif cb in (2, 3):
                        P.act(stg[sq_][:, 0:n], pt[:, 0:n], AF.Identity, [pk], [f"stg{sq_}"])
                        dst = KT_d
                    else:
                        P.act(stg[sq_][:, 0:n], pt[:, 0:n], AF.Silu, [pk], [f"stg{sq_}"])
                        dst = GA_d if cb in (6, 7) else GB_d
                    P.dma("sp", dst[idx, :, t0:t0 + n], stg[sq_][:, 0:n], reads=[f"stg{sq_}"])
        P.barrier()
    with ExitStack() as st:
        sb = lambda name, shape, dt: st.enter_context(nc.sbuf_tensor("s_" + name, shape, dt))
        S = s5_prep(P, nc, sb, s5p_d, iota_d, need_F=False)
        s5_alloc_work(S, sb)
        E = sb("Eout", [128, 2, 64], F32)
        Bb = [[sb(f"Bb{q}{c}", [128, 8, 128], BF16) for c in range(2)] for q in range(2)]
        for ct in range(8):
            q = ct % 2
            P.dma("pool", Bb[q][0][:], Bre_d[ct], writes=[f"Bb{q}0"])
            P.dma("pool", Bb[q][1][:], Bim_d[ct], writes=[f"Bb{q}1"])
            for k in range(2):
                for pp in range(4):
                    e_ = k * 4 + pp
                    kp = k * 32 + 4 * ct + pp
                    s5_tables(P, S, kp, NLAT)
                    usrc = uTb if k == 0 else uTr
                    ukey = f"uTb{ct}" if k == 0 else f"uTr{ct}"
                    s5_drive_scan(P, S, kp, NLAT, Bb[q][0][:, e_, :], Bb[q][1][:, e_, :], [f"Bb{q}0", f"Bb{q}1"],
                                  (lambda src, ct_: (lambda t0, n: src[:, ct_, t0:t0 + n]))(usrc, ct), [ukey], None)
                    s5_end_state(P, S, NLAT, E[:, 0, kp:kp + 1], E[:, 1, kp:kp + 1], "Eout")
        P.dma("sp", E_d, E[:], reads=["Eout"])
    P.emit()
    return nc


NSLOT = 14


def build_b_even():
    nc = bass.Bass("TRN2", target_bir_lowering=False)
    xT_d = _din(nc, "xT", [128, KT, NTOK])
    modT_d = _din(nc, "modT", [128, 2, 3, KT])
    QT_d = _din(nc, "QT", [8, 128, NTOK], BF16)
    GA_d = _din(nc, "GAT", [8, 128, NTOK], BF16)
    GB_d = _din(nc, "GBT", [8, 128, NTOK], BF16)
    UT_d = _din(nc, "UT", [8, 128, NTOK], F32)
    Kl_d = _din(nc, "Kloc", [8, 128, NSLOT * 128], BF16)
    Vl_d = _din(nc, "Vloc", [128, NSLOT, 1024], BF16)
    Kc_d = _din(nc, "KTc", [8, 128, NCTX], BF16)
    Vc_d = _din(nc, "Vc", [128, 2, 1024], BF16)
    BM_d = _din(nc, "BM", [8, 128, 56 * 128], F32)
    id_d = _din(nc, "ident", [128, 128], BF16)
    s5p_d = _din(nc, "s5p", [128, 3, 64])
    iota_d = _din(nc, "iota", [128, 1024])
    Bre_d = _din(nc, "Bre", [8, 128, 8, 128])
    Bim_d = _din(nc, "Bim", [8, 128, 8, 128])
    Cre_d = _din(nc, "Cre", [8, 128, 8, 128])
    Cim_d = _din(nc, "Cim", [8, 128, 8, 128])
    dcol_d = _din(nc, "dcol", [128, 8])
    gw_d = _din(nc, "glu_w", [1024, 1024])
    gb_d = _din(nc, "glub", [128, 8])
    EE_d = _din(nc, "EE", [128, 7, 2, 64])
    sel_d = _din(nc, "sel", [128, 8, 64])
    wo_d = _din(nc, "w_out", [D, D])
    lnT_d = _din(nc, "lnT", [128, 2, KT])
    xo_d = _dout(nc, "xo", [128, KT, NTOK])
    P = Prog(nc)
    sb0 = lambda name, shape, dt: nc.alloc_sbuf_tensor("s_" + name, shape, dt)
    zT = sb0("zT", [128, KT, NTOK], BF16)
    modT = sb0("modT", [128, 2, 3, KT], F32)
    P.dma("sp", modT[:], modT_d, writes=["modT"])
    onesb = sb0("onesb", [128, 128], BF16)
    P.memset("pool", onesb[:], 1.0, ["onesb"])
    with ExitStack() as st:
        sb = lambda name, shape, dt: st.enter_context(nc.sbuf_tensor("s_" + name, shape, dt))
        Q = sb("Q", [128, 8, NTOK], BF16)
        GA = sb("GA", [128, 8, NTOK], BF16)
        Kl = sb("Kl", [128, 8, NSLOT * 128], BF16)
        Vl = sb("Vl", [128, NSLOT, 1024], BF16)
        Kc = sb("Kc", [128, 8, NCTX], BF16)
        Vc = sb("Vc", [128, 2, 1024], BF16)
        ident = sb("ident", [128, 128], BF16)
        BMb = [sb(f"BMb{i}", [128, 56 * 128], BF16) for i in range(2)]
        PT = [sb(f"PT{i}", [128, 256], BF16) for i in range(3)]
        rec = [sb(f"rec{i}", [128, 256], F32) for i in range(2)]
        z1 = [sb(f"z1{i}", [128, 256], F32) for i in range(2)]
        for h in range(8):
            P.dma("sp", Q[:, h, :], QT_d[h], writes=[f"Q{h}"])
            P.dma("sp", GA[:, h, :], GA_d[h], writes=[f"GA{h}"])
            P.dma("sp", Kl[:, h, :], Kl_d[h], writes=[f"Kl{h}"])
            P.dma("sp", Kc[:, h, :], Kc_d[h], writes=[f"Kc{h}"])
        for b in range(NSLOT):
            P.dma("sp", Vl[:, b, :], Vl_d[:, b, :], writes=[f"Vl{b}"])
        P.dma("sp", Vc[:], Vc_d, writes=["Vc"])
        P.dma("sp", ident[:], id_d, writes=["ident"])
        pi = 0
        fi = 0
        for h in range(8):
            bq = h % 2
            for part in range(7):
                P.dma("pool", BMb[bq][:, 1024 * part:1024 * (part + 1)], BM_d[h, :, 1024 * part:1024 * (part + 1)],
                      writes=[f"BMb{bq}"])
            for m in range(9):
                if m < 8:
                    q0, n = 128 * m, 128
                    tiles = [("l", m + i, m * 7 + i) for i in range(7)] + [("c", 0, 0), ("c", 1, 0)]
                else:
                    q0, n = NLAT, NCTX
                    tiles = [("c", 0, 0), ("c", 1, 0)]
                ot, ok = P.bank("O", [4, 5])
                dt_, dk = P.bank("Dn", [6, 7])
                for j, (kind, b, bmi) in enumerate(tiles):
                    s_t, s_k = P.bank("S", [0, 1, 2, 3])
                    if kind == "l":
                        P.mm(s_t[:, 0:n], Kl[:, h, b * 128:(b + 1) * 128], Q[:, h, q0:q0 + n], True, False,
                             [f"Kl{h}", f"Q{h}"], [s_k])
                        P.mm(s_t[:, 0:n], ident[:], BMb[bq][:, bmi * 128:(bmi + 1) * 128], False, True,
                             ["ident", f"BMb{bq}"], [s_k])
                        vap, vk = Vl[:, b, h * 128:(h + 1) * 128], f"Vl{b}"
                    else:
                        P.mm(s_t[:, 0:n], Kc[:, h, b * 128:(b + 1) * 128], Q[:, h, q0:q0 + n], True, True,
                             [f"Kc{h}", f"Q{h}"], [s_k])
                        vap, vk = Vc[:, b, h * 128:(h + 1) * 128], "Vc"
                    p = pi % 3
                    pi += 1
                    P.act(PT[p][:, 0:n], s_t[:, 0:n], AF.Exp, [s_k], [f"PT{p}"])
                    P.mm(ot[:, 0:n], vap, PT[p][:, 0:n], j == 0, j == len(tiles) - 1, [vk, f"PT{p}"], [ok])
                    P.mm(dt_[:, 0:n], onesb[:], PT[p][:, 0:n], j == 0, j == len(tiles) - 1, ["onesb", f"PT{p}"], [dk])
                f = fi % 2
                fi += 1
                P.op("dve", (lambda o_, i_: (lambda e: e.reciprocal(o_, i_)))(rec[f][:, 0:n], dt_[:, 0:n]),
                     [dk], [f"rec{f}"])
                P.tt("dve", z1[f][:, 0:n], ot[:, 0:n], rec[f][:, 0:n], ALU.mult, [ok, f"rec{f}"], [f"z1{f}"])
                P.tt("pool", zT[:, h, q0:q0 + n], z1[f][:, 0:n], GA[:, h, q0:q0 + n], ALU.mult,
                     [f"z1{f}", f"GA{h}"], [f"zT{h}"])
        P.barrier()
    with ExitStack() as st:
        sb_outer = lambda name, shape, dt: st.enter_context(nc.sbuf_tensor("s_" + name, shape, dt))
        yb = sb_outer("yb", [128, 8, NTOK], F32)
        st2 = ExitStack()
        sb = lambda name, shape, dt: st2.enter_context(nc.sbuf_tensor("s_" + name, shape, dt))
        S = s5_prep(P, nc, sb, s5p_d, iota_d, need_F=True)
        s5_alloc_work(S, sb)
        dcol = sb("dcol", [128, 8], F32)
        P.dma("sp", dcol[:], dcol_d, writes=["dcol"])
        Ec = sb("Ec", [128, 2, 64], F32)
        EE = sb("EE", [128, 7, 2, 64], F32)
        sel = sb("sel", [128, 8, 64], F32)
        P.dma("sp", EE[:], EE_d, writes=["EE"])
        P.dma("sp", sel[:], sel_d, writes=["sel"])
        gini = sb("gini", [128, 2, 64], F32)
        uf = [sb(f"uf{i}", [128, NTOK], F32) for i in range(2)]
        ub = [sb(f"ub{i}", [128, NTOK], BF16) for i in range(2)]
        ur = [sb(f"ur{i}", [128, NTOK], BF16) for i in range(2)]
        Bb = [[sb(f"Bb{q}{c}", [128, 8, 128], BF16) for c in range(2)] for q in range(2)]
        Cf = [[sb(f"Cf{q}{c}", [128, 8, 128], F32) for c in range(2)] for q in range(2)]
        C2 = [[sb(f"C2{q}{c}", [128, 8, 128], BF16) for c in range(2)] for q in range(2)]
        ctmp = sb("ctmp", [128, 128], F32)
        hb = [sb(f"hb{i}", [128, 1024], BF16) for i in range(2)]
        t1 = sb("ybt1", [128, 512], F32)

        def segment(T, tok0, init_fn, want_end):
            blocks = [(0, 512), (512, 512)] if T == 1024 else [(0, T)]
            for ct in range(8):
                q = ct % 2
                P.dma("sp", uf[q][:], UT_d[ct], writes=[f"uf{q}"])
                P.copy("dve", ub[q][:], uf[q][:], [f"uf{q}"], [f"ub{q}"])
                P.copy("dve", ur[q][:, 0:T], rev_ap(uf[q], NTOK, tok0, T), [f"uf{q}"], [f"ur{q}"])
                P.dma("pool", Bb[q][0][:], Bre_d[ct], writes=[f"Bb{q}0"])
                P.dma("pool", Bb[q][1][:], Bim_d[ct], writes=[f"Bb{q}1"])
                P.dma("sp", Cf[q][0][:], Cre_d[ct], writes=[f"Cf{q}0"])
                P.dma("sp", Cf[q][1][:], Cim_d[ct], writes=[f"Cf{q}1"])
                for e_ in range(8):
                    kp = (e_ // 4) * 32 + 4 * ct + (e_ % 4)
                    P.ts("pool", ctmp[:], Cf[q][1][:, e_, :], S.Fim[:, kp:kp + 1], None, ALU.mult, None,
                         [f"Cf{q}1", "s5Fim"], ["ctmp"])
                    P.stt("dve", C2[q][0][:, e_, :], Cf[q][0][:, e_, :], S.Fre[:, kp:kp + 1], ctmp[:], ALU.mult, ALU.subtract,
                          [f"Cf{q}0", "s5Fre", "ctmp"], [f"C2{q}0"])
                    P.ts("pool", ctmp[:], Cf[q][0][:, e_, :], S.Fim[:, kp:kp + 1], None, ALU.mult, None,
                         [f"Cf{q}0", "s5Fim"], ["ctmp"])
                    P.stt("dve", C2[q][1][:, e_, :], Cf[q][1][:, e_, :], S.nFre[:, kp:kp + 1], ctmp[:], ALU.mult, ALU.subtract,
                          [f"Cf{q}1", "s5nFre", "ctmp"], [f"C2{q}1"])
                ybanks = {}
                for k in range(2):
                    for bi in range(len(blocks)):
                        ybanks[(k, bi)] = (P.ps[k * 2 + bi], f"ps{k * 2 + bi}")
                for k in range(2):
                    for pp in range(4):
                        e_ = k * 4 + pp
                        kp = k * 32 + 4 * ct + pp
                        s5_tables(P, S, kp, T)
                        if k == 0:
                            ufn = (lambda q_: (lambda t0, n: ub[q_][:, tok0 + t0:tok0 + t0 + n]))(q)
                            ukey = f"ub{q}"
                        else:
                            ufn = (lambda q_: (lambda t0, n: ur[q_][:, t0:t0 + n]))(q)
                            ukey = f"ur{q}"
                        s5_drive_scan(P, S, kp, T, Bb[q][0][:, e_, :], Bb[q][1][:, e_, :], [f"Bb{q}0", f"Bb{q}1"],
                                      ufn, [ukey], init_fn(kp))
                        if want_end:
                            s5_end_state(P, S, T, Ec[:, 0, kp:kp + 1], Ec[:, 1, kp:kp + 1], "Ec")
                        m = S.m
                        for (t0, n) in blocks:
                            P.tt("dve", m[0][:, 0:n], S.g[0][:, t0:t0 + n], S.cs[:, t0:t0 + n], ALU.mult, ["s5g0", "s5cs"], ["s5m0"])
                            P.tt("pool", m[1][:, 0:n], S.g[1][:, t0:t0 + n], S.sn[:, t0:t0 + n], ALU.mult, ["s5g1", "s5sn"], ["s5m1"])
                            P.tt("dve", hb[0][:, t0:t0 + n], m[0][:, 0:n], m[1][:, 0:n], ALU.subtract, ["s5m0", "s5m1"], ["hb0"])
                            P.tt("pool", m[2][:, 0:n], S.g[0][:, t0:t0 + n], S.sn[:, t0:t0 + n], ALU.mult, ["s5g0", "s5sn"], ["s5m2"])
                            P.tt("dve", m[3][:, 0:n], S.g[1][:, t0:t0 + n], S.cs[:, t0:t0 + n], ALU.mult, ["s5g1", "s5cs"], ["s5m3"])
                            P.tt("pool", hb[1][:, t0:t0 + n], m[2][:, 0:n], m[3][:, 0:n], ALU.add, ["s5m2", "s5m3"], ["hb1"])
                        for bi, (t0, n) in enumerate(blocks):
                            yt, yk = ybanks[(k, bi)]
                            P.mm(yt[:, 0:n], C2[q][0][:, e_, :], hb[0][:, t0:t0 + n], pp == 0, False, [f"C2{q}0", "hb0"], [yk])
                            P.mm(yt[:, 0:n], C2[q][1][:, e_, :], hb[1][:, t0:t0 + n], False, pp == 3, [f"C2{q}1", "hb1"], [yk])
                nb = len(blocks)
                for bi, (t0, n) in enumerate(blocks):
                    yf, yfk = ybanks[(0, bi)]
                    ybk_t, ybk = ybanks[(1, nb - 1 - bi)]
                    P.stt("dve", t1[:, 0:n], uf[q][:, tok0 + t0:tok0 + t0 + n], dcol[:, ct:ct + 1], yf[:, 0:n], ALU.mult, ALU.add,
                          [f"uf{q}", "dcol", yfk], ["ybt1"])
                    P.tt("dve", yb[:, ct, tok0 + t0:tok0 + t0 + n], t1[:, 0:n], rev_ap(ybk_t, 512, 0, n), ALU.add,
                         ["ybt1", ybk], [f"yb{ct}"])

        segment(NCTX, NLAT, lambda kp: None, True)
        Sre = sb("chSre", [128, 64], F32)
        Sim = sb("chSim", [128, 64], F32)
        Hre = sb("chHre", [128, 64], F32)
        Him = sb("chHim", [128, 64], F32)
        ca = sb("cha", [128, 64], F32)
        cb_ = sb("chb", [128, 64], F32)
        cc = sb("chc", [128, 64], F32)
        cd = sb("chd", [128, 64], F32)
        P.copy("dve", Sre[:], Ec[:, 0, :], ["Ec"], ["chSre"])
        P.copy("dve", Sim[:], Ec[:, 1, :], ["Ec"], ["chSim"])
        P.tt("dve", Hre[:], Sre[:], sel[:, 0, :], ALU.mult, ["chSre", "sel"], ["chHre"])
        P.tt("dve", Him[:], Sim[:], sel[:, 0, :], ALU.mult, ["chSim", "sel"], ["chHim"])
        for mstep in range(1, 8):
            P.tt("dve", ca[:], S.Are[:], Sre[:], ALU.mult, ["s5Are", "chSre"], ["cha"])
            P.tt("dve", cb_[:], S.Aim[:], Sim[:], ALU.mult, ["s5Aim", "chSim"], ["chb"])
            P.tt("dve", cc[:], S.Are[:], Sim[:], ALU.mult, ["s5Are", "chSim"], ["chc"])
            P.tt("dve", cd[:], S.Aim[:], Sre[:], ALU.mult, ["s5Aim", "chSre"], ["chd"])
            P.tt("dve", ca[:], ca[:], cb_[:], ALU.subtract, ["cha", "chb"], ["cha"])
            P.tt("dve", cc[:], cc[:], cd[:], ALU.add, ["chc", "chd"], ["chc"])
            P.tt("dve", Sre[:], ca[:], EE[:, mstep - 1, 0, :], ALU.add, ["cha", "EE"], ["chSre"])
            P.tt("dve", Sim[:], cc[:], EE[:, mstep - 1, 1, :], ALU.add, ["chc", "EE"], ["chSim"])
            P.tt("dve", ca[:], Sre[:], sel[:, mstep, :], ALU.mult, ["chSre", "sel"], ["cha"])
            P.tt("dve", Hre[:], Hre[:], ca[:], ALU.add, ["chHre", "cha"], ["chHre"])
            P.tt("dve", cc[:], Sim[:], sel[:, mstep, :], ALU.mult, ["chSim", "sel"], ["chc"])
            P.tt("dve", Him[:], Him[:], cc[:], ALU.add, ["chHim", "chc"], ["chHim"])
        P.tt("dve", ca[:], S.c1[:], Hre[:], ALU.mult, ["s5oc", "chHre"], ["cha"])
        P.tt("dve", cb_[:], S.s1[:], Him[:], ALU.mult, ["s5os", "chHim"], ["chb"])
        P.tt("dve", gini[:, 0, :], ca[:], cb_[:], ALU.subtract, ["cha", "chb"], ["gini"])
        P.tt("dve", cc[:], S.s1[:], Hre[:], ALU.mult, ["s5os", "chHre"], ["chc"])
        P.tt("dve", cd[:], S.c1[:], Him[:], ALU.mult, ["s5oc", "chHim"], ["chd"])
        P.tt("dve", gini[:, 1, :], cc[:], cd[:], ALU.add, ["chc", "chd"], ["gini"])
        segment(NLAT, 0, lambda kp: (gini[:, 0, kp:kp + 1], gini[:, 1, kp:kp + 1], ["gini"]), False)
        P.barrier()
        st2.close()
        sb = sb_outer
        gw = sb("gw", [128, 8, 1024], BF16)
        for kk in range(2):
            P.dma("pool", gw[:, 4 * kk:4 * kk + 4, :],
                  gw_d[512 * kk:512 * (kk + 1), :].rearrange("(kt p) c -> p kt c", p=128), writes=[f"gw{kk}"])
        glub = sb("glub", [128, 8], F32)
        P.dma("sp", glub[:], gb_d, writes=["glub"])
        zbb = sb("zbb", [128, 8, NTOK], BF16)
        g1 = [sb(f"g1{i}", [128, NTOK], F32) for i in range(2)]
        for ct in range(8):
            q = ct % 2
            P.tt("pool", g1[q][:], yb[:, ct, :], yb[:, ct, :], ALU.mult, [f"yb{ct}"], [f"g1{q}"])
            P.ts("pool", g1[q][:], g1[q][:], 0.044715, 1.0, ALU.mult, ALU.add, [f"g1{q}"], [f"g1{q}"])
            P.tt("pool", g1[q][:], g1[q][:], yb[:, ct, :], ALU.mult, [f"g1{q}", f"yb{ct}"], [f"g1{q}"])
            P.act(g1[q][:], g1[q][:], AF.Sigmoid, [f"g1{q}"], [f"g1{q}"], scale=1.5957691216057308)
            P.tt("dve", yb[:, ct, :], yb[:, ct, :], g1[q][:], ALU.mult, [f"yb{ct}", f"g1{q}"], [f"yb{ct}"])
            P.copy("dve", zbb[:, ct, :], yb[:, ct, :], [f"yb{ct}"], [f"zbb{ct}"])
        GBs = [sb(f"GBs{i}", [128, NTOK], BF16) for i in range(2)]
        sg = [sb(f"sg{i}", [128, 512], F32) for i in range(2)]
        gi_ = 0
        for co in range(8):
            q = co % 2
            P.dma("sp", GBs[q][:], GB_d[co], writes=[f"GBs{q}"])
            for (t0, n) in TB:
                pt, pk = P.bank("glu", [4, 5, 6, 7])
                for ci in range(8):
                    P.mm(pt[:, 0:n], gw[:, ci, co * 128:(co + 1) * 128], zbb[:, ci, t0:t0 + n], ci == 0, ci == 7,
                         [f"gw{ci // 4}", f"zbb{ci}"], [pk])
                f = gi_ % 2
                gi_ += 1
                P.act(sg[f][:, 0:n], pt[:, 0:n], AF.Sigmoid, [pk, "glub"], [f"sg{f}"], bias=glub[:, co:co + 1])
                P.tt("dve", sg[f][:, 0:n], sg[f][:, 0:n], yb[:, co, t0:t0 + n], ALU.mult, [f"sg{f}", f"yb{co}"], [f"sg{f}"])
                P.tt("pool", zT[:, 8 + co, t0:t0 + n], sg[f][:, 0:n], GBs[q][:, t0:t0 + n], ALU.mult,
                     [f"sg{f}", f"GBs{q}"], [f"zT{8 + co}"])
        P.barrier()
    with ExitStack() as st:
        sb = lambda name, shape, dt: st.enter_context(nc.sbuf_tensor("s_" + name, shape, dt))
        xT = sb("xT", [128, KT, NTOK], F32)
        for kk in range(4):
            P.dma("sp", xT[:, 4 * kk:4 * kk + 4, :], xT_d[:, 4 * kk:4 * kk + 4, :], writes=[f"xT{kk}"])
        outproj_ln(P, nc, st, zT, xT, modT, wo_d, lnT_d, xo_d)
    P.emit()
    return nc


def even_host_prep(a_re, a_im, log_dt, b_re, b_im, c_re, c_im, d, glu_b):
    def colmajor(a):
        return a.reshape(2, 32, 2, 64).transpose(2, 3, 0, 1).reshape(128, 64)
    dtl = np.broadcast_to(log_dt.reshape(2, 32, 2).transpose(2, 0, 1)[:, None, :, :], (2, 64, 2, 32)).reshape(128, 64)
    s5p = np.ascontiguousarray(np.stack([colmajor(a_re), colmajor(a_im), dtl], 1).astype(np.float32))
    Bre = np.zeros((8, 128, 8, 128), np.float32)
    Bim = np.zeros((8, 128, 8, 128), np.float32)
    Cre = np.zeros((8, 128, 8, 128), np.float32)
    Cim = np.zeros((8, 128, 8, 128), np.float32)
    for ct in range(8):
        for k in range(2):
            for pp in range(4):
                for g2 in range(2):
                    g = 8 * ct + 2 * pp + g2
                    g8 = 2 * pp + g2
                    e = k * 4 + pp
                    Bre[ct, g8 * 16:(g8 + 1) * 16, e, g2 * 64:(g2 + 1) * 64] = b_re[k, g].T
                    Bim[ct, g8 * 16:(g8 + 1) * 16, e, g2 * 64:(g2 + 1) * 64] = b_im[k, g].T
                    Cre[ct, g2 * 64:(g2 + 1) * 64, e, g8 * 16:(g8 + 1) * 16] = c_re[k, g].T
                    Cim[ct, g2 * 64:(g2 + 1) * 64, e, g8 * 16:(g8 + 1) * 16] = c_im[k, g].T
    return dict(s5p=s5p, Bre=Bre, Bim=Bim, Cre=Cre, Cim=Cim,
                dcol=np.ascontiguousarray(d.reshape(8, 128).T), glub=np.ascontiguousarray(glu_b.reshape(8, 128).T),
                iota=np.ascontiguousarray(np.tile(np.arange(1024, dtype=np.float32), (128, 1))))


def build_BM(rpb, j):
    m = np.arange(8)[:, None, None, None]
    i = np.arange(7)[None, :, None, None]
    kr2 = np.arange(2)[None, None, :, None]
    qr2 = np.arange(2)[None, None, None, :]
    r = 16 * j + 2 * m + qr2
    gt = 8 * j - 3 + m + i
    kr = 2 * gt + kr2
    rs = np.clip(r - 4, 0, 120)
    vrow = (gt >= 0) & (gt < 64) & (kr >= rs) & (kr < rs + 8)
    dr = np.clip(kr - r + 7, 0, 14) + 0 * vrow
    kc = np.arange(64)[:, None]
    qc = np.arange(64)[None, :]
    cs = np.clip(qc - 8, 0, 48)
    vcol = (kc >= cs) & (kc < cs + 16)
    dc = np.clip(kc - qc + 15, 0, 30)
    drb = np.broadcast_to(dr.transpose(2, 0, 1, 3)[:, None, :, :, :, None], (2, 64, 8, 7, 2, 64))
    dcb = np.broadcast_to(dc[None, :, None, None, None, :], (2, 64, 8, 7, 2, 64))
    val = (np.broadcast_to(vrow.transpose(2, 0, 1, 3)[:, None, :, :, :, None], (2, 64, 8, 7, 2, 64))
           & np.broadcast_to(vcol[None, :, None, None, None, :], (2, 64, 8, 7, 2, 64)))
    g = rpb[:, drb, dcb]
    out = np.where(val[None], g, np.float32(-30000.0)).astype(np.float32)
    return np.ascontiguousarray(out.reshape(8, 128, 56 * 128))


def run_even_layer(xTs, modT, w_in, w_out, lnT, prep, rpb, glu_w):
    in_maps = [{"xT": xTs[j], "modT": modT, "w_in": w_in, "s5p": prep["s5p"], "iota": prep["iota"],
                "Bre": prep["Bre"], "Bim": prep["Bim"]} for j in range(NCORE)]
    ra = _run("a_even", build_a_even, in_maps)
    KT_all = np.concatenate([r["KTo"][:, :, :NLAT] for r in ra], axis=2)
    V_all = np.concatenate([r["V"][:NLAT] for r in ra], axis=0)
    KTc = np.ascontiguousarray(ra[0]["KTo"][:, :, NLAT:])
    Vc = np.ascontiguousarray(ra[0]["V"][NLAT:].reshape(2, 128, 1024).transpose(1, 0, 2))
    ident = np.eye(128, dtype=np.float32).astype(BF)
    E_all = [r["E"] for r in ra]
    EE = np.zeros((128, 7, 2, 64), np.float32)
    for m in range(1, 8):
        EE[:, m - 1, :, 0:32] = E_all[m - 1][:, :, 0:32]
        EE[:, m - 1, :, 32:64] = E_all[8 - m][:, :, 32:64]
    in_maps = []
    for j in range(NCORE):
        Kloc = np.zeros((8, 128, NSLOT * 128), BF)
        Vloc = np.zeros((128, NSLOT, 1024), BF)
        for b in range(NSLOT):
            gt = 8 * j - 3 + b
            if 0 <= gt < 64:
                Kloc[:, :, b * 128:(b + 1) * 128] = KT_all[:, :, gt * 128:(gt + 1) * 128]
                Vloc[:, b, :] = V_all[gt * 128:(gt + 1) * 128, :]
        sel = np.zeros((128, 8, 64), np.float32)
        sel[:, j, 0:32] = 1.0
        sel[:, 7 - j, 32:64] = 1.0
        in_maps.append({"xT": xTs[j], "modT": modT, "QT": ra[j]["QT"], "GAT": ra[j]["GAT"], "GBT": ra[j]["GBT"],
                        "UT": ra[j]["UT"], "Kloc": Kloc, "Vloc": Vloc, "KTc": KTc, "Vc": Vc, "BM": build_BM(rpb, j),
                        "ident": ident, "s5p": prep["s5p"], "iota": prep["iota"], "Bre": prep["Bre"], "Bim": prep["Bim"],
                        "Cre": prep["Cre"], "Cim": prep["Cim"], "dcol": prep["dcol"], "glu_w": glu_w, "glub": prep["glub"],
                        "EE": EE, "sel": sel, "w_out": w_out, "lnT": lnT})
    rb = _run("b_even", build_b_even, in_maps)
    return [r["xo"] for r in rb], ra


def kernel(x, c, ctx, c_ctx, ada_w, ada_b, ln_g, ln_b, ev_w_in, ev_w_out, na_rpb,
           s5_a_re, s5_a_im, s5_log_dt, s5_b_re, s5_b_im, s5_c_re, s5_c_im, s5_d,
           s5_glu_w, s5_glu_b, od_w_in, od_w_out, q_norm_g, k_norm_g):
    f = lambda a: np.ascontiguousarray(np.asarray(a, dtype=np.float32))
    x, c, ctx, c_ctx, ada_w, ada_b, ln_g, ln_b = map(f, (x, c, ctx, c_ctx, ada_w, ada_b, ln_g, ln_b))
    ev_w_in, ev_w_out, na_rpb, od_w_in, od_w_out, q_norm_g, k_norm_g = map(
        f, (ev_w_in, ev_w_out, na_rpb, od_w_in, od_w_out, q_norm_g, k_norm_g))
    s5_a_re, s5_a_im, s5_log_dt, s5_b_re, s5_b_im, s5_c_re, s5_c_im, s5_d, s5_glu_w, s5_glu_b = map(
        f, (s5_a_re, s5_a_im, s5_log_dt, s5_b_re, s5_b_im, s5_c_re, s5_c_im, s5_d, s5_glu_w, s5_glu_b))
    mod = host_mod(c, c_ctx, ada_w, ada_b)
    X = x[0]
    XC = ctx[0]
    xTs = [to_fm(np.concatenate([X[NLAT * j:NLAT * (j + 1)], XC], 0)) for j in range(NCORE)]
    for l in range(4):
        i = l // 2
        modT = mod_to_T(mod[l])
        lnT = ln_to_T(ln_g[l], ln_b[l])
        if l % 2 == 0:
            prep = even_host_prep(s5_a_re[i], s5_a_im[i], s5_log_dt[i], s5_b_re[i], s5_b_im[i],
                                  s5_c_re[i], s5_c_im[i], s5_d[i], s5_glu_b[i])
            xTs, _ = run_even_layer(xTs, modT, ev_w_in[i], ev_w_out[i], lnT, prep, na_rpb[i], s5_glu_w[i])
        else:
            xTs, _ = run_odd_layer(xTs, modT, od_w_in[i], od_w_out[i], q_norm_g[i], k_norm_g[i], lnT)
    out = np.concatenate([from_fm(xTs[j][:, :, :NLAT]) for j in range(NCORE)], 0)
    return np.ascontiguousarray(out.reshape(1, SEQ, D).astype(np.float32))
```

```python
import math
from contextlib import ExitStack
import numpy as np
import ml_dtypes
import concourse.bass as bass
import concourse.mybir as mybir
from concourse.bass_utils import run_bass_kernel_spmd

F32 = mybir.dt.float32
BF16 = mybir.dt.bfloat16
AF = mybir.ActivationFunctionType
ALU = mybir.AluOpType

NCORE = 8
D = 2048
KT = 16
NLAT = 1024
NCTX = 256
NTOK = NLAT + NCTX
TB = [(0, 512), (512, 512), (1024, 256)]
SEQ = 8192
DN_ALPHA = 8.0 ** 0.25
LN_EPS = 1e-6
RMS_EPS = 1e-6
ENGS = ("pe", "act", "dve", "pool", "sp")
NDMA_SEMS = 24
TWO_PI = 2.0 * math.pi


class Prog:
    def __init__(self, nc):
        self.nc = nc
        self.ops = []
        self.last_writer = {}
        self.readers = {}
        self.n_dma = 0
        self.bar_deps = set()
        self.bar_applied = set(ENGS)
        self.bar_start = 0
        self.ps = [nc.alloc_psum_tensor(f"psb{i}", [128, 512], F32) for i in range(8)]
        self.rot = {}

    def bank(self, group, ids):
        i = self.rot.get(group, 0)
        self.rot[group] = i + 1
        b = ids[i % len(ids)]
        return self.ps[b], f"ps{b}"

    def _deps(self, eng, reads, writes):
        deps = set()
        for k in reads:
            w = self.last_writer.get(k)
            if w is not None:
                deps.add(w)
        for k in writes:
            w = self.last_writer.get(k)
            if w is not None:
                deps.add(w)
            for r in self.readers.get(k, ()):
                deps.add(r)
        if eng not in self.bar_applied:
            deps |= self.bar_deps
            self.bar_applied.add(eng)
        return deps

    def _commit(self, idx, reads, writes):
        for k in reads:
            self.readers.setdefault(k, []).append(idx)
        for k in writes:
            self.last_writer[k] = idx
            self.readers[k] = []

    def barrier(self):
        last = {}
        deps = set()
        for i in range(self.bar_start, len(self.ops)):
            o = self.ops[i]
            if o["dma"]:
                deps.add(i)
            else:
                last[o["eng"]] = i
        deps |= set(last.values())
        self.bar_deps = deps
        self.bar_applied = set()
        self.bar_start = len(self.ops)

    def op(self, eng, fn, reads=(), writes=()):
        idx = len(self.ops)
        deps = self._deps(eng, reads, writes)
        self.ops.append(dict(eng=eng, fn=fn, deps=deps, dma=False, signal=False))
        self._commit(idx, reads, writes)
        return idx

    def dma(self, eng, out, in_, reads=(), writes=(), **kw):
        idx = len(self.ops)
        deps = self._deps(eng, reads, writes)
        n = self.n_dma
        self.n_dma += 1
        self.ops.append(dict(eng=eng, fn=None, out=out, in_=in_, kw=kw, deps=deps,
                             dma=True, dma_n=n, signal=True))
        self._commit(idx, reads, writes)
        return idx

    def mm(self, out, lhsT, rhs, start, stop, reads, writes):
        return self.op("pe", lambda e: e.matmul(out, lhsT, rhs, start=start, stop=stop), reads, writes)

    def act(self, out, in_, func, reads, writes, scale=1.0, bias=0.0):
        return self.op("act", lambda e: e.activation(out, in_, func, bias=bias, scale=scale), reads, writes)

    def tt(self, eng, out, a, b, op, reads, writes):
        return self.op(eng, lambda e: e.tensor_tensor(out, a, b, op), reads, writes)

    def ts(self, eng, out, a, s1, s2, op0, op1, reads, writes):
        if s2 is None:
            return self.op(eng, lambda e: e.tensor_scalar(out, a, s1, None, op0), reads, writes)
        return self.op(eng, lambda e: e.tensor_scalar(out, a, s1, s2, op0, op1), reads, writes)

    def stt(self, eng, out, in0, scalar, in1, op0, op1, reads, writes):
        return self.op(eng, lambda e: e.scalar_tensor_tensor(out, in0, scalar, in1, op0, op1), reads, writes)

    def copy(self, eng, out, in_, reads, writes):
        return self.op(eng, lambda e: e.tensor_copy(out, in_), reads, writes)

    def memset(self, eng, ap, val, writes):
        return self.op(eng, lambda e: e.memset(ap, val), (), writes)

    def emit(self):
        nc = self.nc
        ops = self.ops

        def nosync(od, o, engname):
            return od["eng"] == engname and (not o["dma"]) and engname == "pe"

        for i, o in enumerate(ops):
            for d in o["deps"]:
                od = ops[d]
                if od["dma"]:
                    continue
                if nosync(od, o, o["eng"]):
                    continue
                od["signal"] = True
        cnt = {e: 0 for e in ENGS}
        for o in ops:
            if o["dma"]:
                n = o["dma_n"]
                o["sem_id"] = n % NDMA_SEMS
                o["target"] = 16 * (n // NDMA_SEMS + 1)
            elif o["signal"]:
                cnt[o["eng"]] += 1
                o["count"] = cnt[o["eng"]]
        esems = {e: nc.alloc_semaphore(f"s_{e}") for e in ENGS}
        dsems = [nc.alloc_semaphore(f"s_dma{j}") for j in range(NDMA_SEMS)]
        by_eng = {e: [] for e in ENGS}
        for i, o in enumerate(ops):
            by_eng[o["eng"]].append(i)

        def run(engname, engine):
            waited = {}
            for i in by_eng[engname]:
                o = ops[i]
                need = {}
                for d in o["deps"]:
                    od = ops[d]
                    if od["dma"]:
                        key = ("d", od["sem_id"])
                        val = od["target"]
                    else:
                        if nosync(od, o, engname):
                            continue
                        key = ("e", od["eng"])
                        val = od["count"]
                    if val > need.get(key, 0):
                        need[key] = val
                if o["dma"]:
                    prev = o["target"] - 16
                    if prev > 0:
                        key = ("d", o["sem_id"])
                        if prev > need.get(key, 0):
                            need[key] = prev
                for key, val in need.items():
                    if waited.get(key, 0) >= val:
                        continue
                    waited[key] = val
                    sem = dsems[key[1]] if key[0] == "d" else esems[key[1]]
                    engine.wait_ge(sem, val)
                if o["dma"]:
                    ins = engine.dma_start(out=o["out"], in_=o["in_"], **o["kw"])
                    ins.then_inc(dsems[o["sem_id"]], 16)
                else:
                    ins = o["fn"](engine)
                    if o["signal"]:
                        ins.then_inc(esems[engname], 1)

        with nc.Block() as block:
            @block.tensor
            def _(e):
                run("pe", e)

            @block.scalar
            def _(e):
                run("act", e)

            @block.vector
            def _(e):
                run("dve", e)

            @block.gpsimd
            def _(e):
                run("pool", e)

            @block.sync
            def _(e):
                run("sp", e)
                final = {}
                for o in ops:
                    if o["dma"]:
                        final[o["sem_id"]] = max(final.get(o["sem_id"], 0), o["target"])
                for sid, val in final.items():
                    e.wait_ge(dsems[sid], val)


def _din(nc, name, shape, dt=F32):
    return nc.dram_tensor(name, list(shape), dt, kind="ExternalInput").ap()


def _dout(nc, name, shape, dt=F32):
    return nc.dram_tensor(name, list(shape), dt, kind="ExternalOutput").ap()


def build_mod():
    nc = bass.Bass("TRN2", target_bir_lowering=False)
    cT = _din(nc, "cT", [128, 16, 2])
    aw = _din(nc, "aw", [4, 2048, 768])
    ab = _din(nc, "ab", [4, 768])
    out = _dout(nc, "mod", [4, 2, 768])
    P = Prog(nc)
    sb = lambda name, shape, dt: nc.alloc_sbuf_tensor("s_" + name, shape, dt)
    c_sb = sb("c_sb", [128, 16, 2], F32)
    s_sb = sb("s_sb", [128, 16, 2], F32)
    P.dma("sp", c_sb[:], cT, writes=["c"])
    P.act(s_sb[:], c_sb[:], AF.Silu, ["c"], ["s"])
    wbuf = [sb(f"w{i}", [128, 16, 768], F32) for i in range(2)]
    bias = [sb(f"b{i}", [2, 768], F32) for i in range(2)]
    osb = [sb(f"o{i}", [2, 768], F32) for i in range(2)]
    for l in range(4):
        q = l % 2
        for kk in range(4):
            P.dma("sp", wbuf[q][:, 4 * kk:4 * kk + 4, :],
                  aw[l, 512 * kk:512 * (kk + 1), :].rearrange("(kt p) c -> p kt c", p=128), writes=[f"w{q}_{kk}"])
        for r in range(2):
            P.dma("sp", bias[q][r:r + 1, :], ab[l:l + 1, :], writes=[f"b{q}"])
        for (c0, cn) in ((0, 512), (512, 256)):
            pt, pk = P.bank("m", [0, 1])
            for kt in range(16):
                P.mm(pt[0:2, 0:cn], s_sb[:, kt, :], wbuf[q][:, kt, c0:c0 + cn], kt == 0, kt == 15,
                     ["s", f"w{q}_{kt // 4}"], [pk])
            P.tt("dve", osb[q][:, c0:c0 + cn], pt[0:2, 0:cn], bias[q][:, c0:c0 + cn], ALU.add,
                 [pk, f"b{q}"], [f"o{q}"])
        P.dma("sp", out[l], osb[q][:], reads=[f"o{q}"])
    P.emit()
    return nc


def load_x_and_mod(P, nc, xT_d, modT_d):
    sb = lambda name, shape, dt: nc.alloc_sbuf_tensor("s_" + name, shape, dt)
    xT = sb("xT", [128, KT, NTOK], F32)
    modT = sb("modT", [128, 2, 3, KT], F32)
    P.dma("sp", modT[:], modT_d, writes=["modT"])
    for kk in range(4):
        P.dma("sp", xT[:, 4 * kk:4 * kk + 4, :], xT_d[:, 4 * kk:4 * kk + 4, :], writes=[f"xT{kk}"])
    return xT, modT


def make_hT(P, nc, xT, modT):
    sb = lambda name, shape, dt: nc.alloc_sbuf_tensor("s_" + name, shape, dt)
    sc1 = sb("sc1", [128, 2, KT], F32)
    P.ts("dve", sc1[:], modT[:, :, 1, :], 1.0, None, ALU.add, None, ["modT"], ["sc1"])
    hT = sb("hT", [128, KT, NTOK], BF16)
    for kt in range(KT):
        for s, (t0, n) in ((0, (0, NLAT)), (1, (NLAT, NCTX))):
            P.act(hT[:, kt, t0:t0 + n], xT[:, kt, t0:t0 + n], AF.Identity,
                  [f"xT{kt // 4}", "sc1", "modT"], [f"hT{kt}"],
                  scale=sc1[:, s, kt:kt + 1], bias=modT[:, s, 0, kt:kt + 1])
    return hT


def stream_weight_blocks(P, nc, w_d, nblk, name):
    bufs = [nc.alloc_sbuf_tensor(f"s_{name}{i}", [128, KT, 512], BF16) for i in range(2)]
    issued = {}

    def blk(i):
        if i not in issued:
            q = i % 2
            for kk in range(4):
                P.dma("pool", bufs[q][:, 4 * kk:4 * kk + 4, :],
                      w_d[512 * kk:512 * (kk + 1), 512 * i:512 * (i + 1)].rearrange("(kt p) c -> p kt c", p=128),
                      writes=[f"{name}{q}_{kk}"])
            issued[i] = True
        return bufs[i % 2], f"{name}{i % 2}"
    return blk


def lin_fm(P, wb, wkey, c, hT, hkey, t0, n, pt, pk):
    for kt in range(KT):
        P.mm(pt[:, 0:n], wb[:, kt, c * 128:(c + 1) * 128], hT[:, kt, t0:t0 + n], kt == 0, kt == KT - 1,
             [f"{wkey}_{kt // 4}", f"{hkey}{kt}"], [pk])


def outproj_ln(P, nc, st, zT, xT, modT, w_out_d, lnT_d, xo_d):
    sb = lambda name, shape, dt: st.enter_context(nc.sbuf_tensor("s_" + name, shape, dt))
    wo = sb("wo", [128, KT, D], BF16)
    for cb in range(4):
        for kk in range(4):
            P.dma("pool", wo[:, 4 * kk:4 * kk + 4, 512 * cb:512 * (cb + 1)],
                  w_out_d[512 * kk:512 * (kk + 1), 512 * cb:512 * (cb + 1)].rearrange("(kt p) c -> p kt c", p=128),
                  writes=[f"wo{cb}_{kk}"])
    lnT = sb("lnT", [128, 2, KT], F32)
    P.dma("sp", lnT[:], lnT_d, writes=["lnT"])
    ones32 = sb("ones32b", [128, 128], F32)
    P.memset("pool", ones32[:], 1.0, ["ones32b"])
    epsc = sb("epsc_ln", [128, 1], F32)
    P.memset("pool", epsc[:], LN_EPS, ["epsc_ln"])
    ax = [sb(f"ax{i}", [128, 512], F32) for i in range(2)]
    sq = [sb(f"sq{i}", [128, 512], F32) for i in range(2)]
    mean = sb("mean", [128, 512], F32)
    msq = sb("msq", [128, 512], F32)
    rstd = sb("rstd", [128, 512], F32)
    tmp = [sb(f"lt{i}", [128, 512], F32) for i in range(2)]
    for bi, (t0, n) in enumerate(TB):
        s = 0 if t0 < NLAT else 1
        s1t, s1k = P.bank("ln1", [4, 5])
        s2t, s2k = P.bank("ln2", [6, 7])
        for fb in range(KT):
            pt, pk = P.bank("op", [0, 1, 2, 3])
            for kt in range(KT):
                P.mm(pt[:, 0:n], wo[:, kt, fb * 128:(fb + 1) * 128], zT[:, kt, t0:t0 + n], kt == 0, kt == KT - 1,
                     [f"wo{fb // 4}_{kt // 4}", f"zT{kt}"], [pk])
            q = fb % 2
            xk = f"xT{fb // 4}"
            P.act(ax[q][:, 0:n], xT[:, fb, t0:t0 + n], AF.Identity, [xk], [f"ax{q}"], scale=DN_ALPHA)
            P.stt("dve", xT[:, fb, t0:t0 + n], pt[:, 0:n], modT[:, s, 2, fb:fb + 1], ax[q][:, 0:n],
                  ALU.mult, ALU.add, [pk, "modT", f"ax{q}"], [xk])
            P.act(sq[q][:, 0:n], xT[:, fb, t0:t0 + n], AF.Square, [xk], [f"sq{q}"])
            P.mm(s1t[:, 0:n], ones32[:], xT[:, fb, t0:t0 + n], fb == 0, fb == KT - 1, ["ones32b", xk], [s1k])
            P.mm(s2t[:, 0:n], ones32[:], sq[q][:, 0:n], fb == 0, fb == KT - 1, ["ones32b", f"sq{q}"], [s2k])
        P.ts("dve", mean[:, 0:n], s1t[:, 0:n], 1.0 / D, None, ALU.mult, None, [s1k], ["mean"])
        P.tt("dve", msq[:, 0:n], mean[:, 0:n], mean[:, 0:n], ALU.mult, ["mean"], ["msq"])
        P.stt("dve", rstd[:, 0:n], s2t[:, 0:n], 1.0 / D, msq[:, 0:n], ALU.mult, ALU.subtract, [s2k, "msq"], ["rstd"])
        P.act(rstd[:, 0:n], rstd[:, 0:n], AF.Sqrt, ["rstd", "epsc_ln"], ["rstd"], bias=epsc[:, 0:1])
        P.op("dve", (lambda o_, i_: (lambda e: e.reciprocal(o_, i_)))(rstd[:, 0:n], rstd[:, 0:n]), ["rstd"], ["rstd"])
        for fb in range(KT):
            q = fb % 2
            xk = f"xT{fb // 4}"
            eng = "dve"
            P.tt(eng, tmp[q][:, 0:n], xT[:, fb, t0:t0 + n], mean[:, 0:n], ALU.subtract, [xk, "mean"], [f"lt{q}"])
            P.tt(eng, tmp[q][:, 0:n], tmp[q][:, 0:n], rstd[:, 0:n], ALU.mult, [f"lt{q}", "rstd"], [f"lt{q}"])
            P.ts(eng, xT[:, fb, t0:t0 + n], tmp[q][:, 0:n], lnT[:, 0, fb:fb + 1], lnT[:, 1, fb:fb + 1],
                 ALU.mult, ALU.add, [f"lt{q}", "lnT"], [xk])
        for kk in range(4):
            P.dma("sp", xo_d[:, 4 * kk:4 * kk + 4, t0:t0 + n], xT[:, 4 * kk:4 * kk + 4, t0:t0 + n], reads=[f"xT{kk}"])


def build_a_odd():
    nc = bass.Bass("TRN2", target_bir_lowering=False)
    xT_d = _din(nc, "xT", [128, KT, NTOK])
    modT_d = _din(nc, "modT", [128, 2, 3, KT])
    w_d = _din(nc, "w_in", [D, 5120])
    cos_d = _din(nc, "cosT", [128, NTOK])
    sin_d = _din(nc, "sinT", [128, NTOK])
    g_d = _din(nc, "gcol", [128, 2])
    perm_d = _din(nc, "permT", [128, 128])
    QT_d = _dout(nc, "QT", [16, 128, NTOK], BF16)
    KT_d = _dout(nc, "KTo", [4, 128, NTOK], BF16)
    V_d = _dout(nc, "V", [NTOK, 512], BF16)
    GT_d = _dout(nc, "GT", [16, 128, NTOK], BF16)
    P = Prog(nc)
    sb = lambda name, shape, dt: nc.alloc_sbuf_tensor("s_" + name, shape, dt)
    xT, modT = load_x_and_mod(P, nc, xT_d, modT_d)
    hT = make_hT(P, nc, xT, modT)
    cosT = sb("cosT", [128, NTOK], F32)
    sinT = sb("sinT", [128, NTOK], F32)
    gcol = sb("gcol", [128, 2], F32)
    permT = sb("permT", [128, 128], F32)
    ones32 = sb("ones32", [128, 128], F32)
    P.dma("sp", cosT[:], cos_d, writes=["cosT"])
    P.dma("sp", sinT[:], sin_d, writes=["sinT"])
    P.dma("sp", gcol[:], g_d, writes=["gcol"])
    P.dma("sp", permT[:], perm_d, writes=["permT"])
    P.memset("pool", ones32[:], 1.0, ["ones32"])
    epsr = sb("epsr", [128, 1], F32)
    P.memset("pool", epsr[:], 128.0 * RMS_EPS, ["epsr"])
    P.ts("dve", gcol[:, 1:2], gcol[:, 1:2], math.sqrt(128.0), None, ALU.mult, None, ["gcol"], ["gcol"])
    blk = stream_weight_blocks(P, nc, w_d, 10, "wi")
    NB = 3
    sqb = [sb(f"sqb{i}", [128, 512], F32) for i in range(NB)]
    rsb = [sb(f"rsb{i}", [128, 512], F32) for i in range(NB)]
    qnb = [sb(f"qnb{i}", [128, 512], F32) for i in range(NB)]
    t1b = [sb(f"t1b{i}", [128, 512], F32) for i in range(NB)]
    t2b = [sb(f"t2b{i}", [128, 512], F32) for i in range(NB)]
    stg = [sb(f"stg{i}", [128, 512], BF16) for i in range(4)]
    vst = [sb(f"vst{i}", [128, 512], BF16) for i in range(2)]
    it = 0
    si = 0
    for cb in range(10):
        wb, wkey = blk(cb)
        if cb + 1 < 10:
            blk(cb + 1)
        if cb == 5:
            for tt in range(NTOK // 128):
                pt, pk = P.bank("main", [0, 1, 2, 3])
                for kt in range(KT):
                    P.mm(pt[:, :], hT[:, kt, tt * 128:(tt + 1) * 128], wb[:, kt, :], kt == 0, kt == KT - 1,
                         [f"{wkey}_{kt // 4}", f"hT{kt}"], [pk])
                q = tt % 2
                P.act(vst[q][:], pt[:, :], AF.Identity, [pk], [f"vst{q}"])
                P.dma("sp", V_d[tt * 128:(tt + 1) * 128, :], vst[q][:], reads=[f"vst{q}"])
            continue
        for c in range(4):
            for (t0, n) in TB:
                pt, pk = P.bank("main", [0, 1, 2, 3])
                lin_fm(P, wb, wkey, c, hT, "hT", t0, n, pt, pk)
                sq_ = si % 4
                si += 1
                if cb >= 6:
                    head = (cb - 6) * 4 + c
                    P.act(stg[sq_][:, 0:n], pt[:, 0:n], AF.Silu, [pk], [f"stg{sq_}"])
                    P.dma("sp", GT_d[head, :, t0:t0 + n], stg[sq_][:, 0:n], reads=[f"stg{sq_}"])
                    continue
                isq = cb < 4
                head = cb * 4 + c if isq else c
                gc = gcol[:, 0:1] if isq else gcol[:, 1:2]
                b = it % NB
                it += 1
                P.act(sqb[b][:, 0:n], pt[:, 0:n], AF.Square, [pk], [f"sqb{b}"])
                st_, sk = P.bank("ss", [4, 5])
                P.mm(st_[:, 0:n], ones32[:], sqb[b][:, 0:n], True, True, ["ones32", f"sqb{b}"], [sk])
                P.act(rsb[b][:, 0:n], st_[:, 0:n], AF.Sqrt, [sk, "epsr"], [f"rsb{b}"], bias=epsr[:, 0:1])
                P.op("dve", (lambda o_, i_: (lambda e: e.reciprocal(o_, i_)))(rsb[b][:, 0:n], rsb[b][:, 0:n]), [f"rsb{b}"], [f"rsb{b}"])
                P.stt("dve", qnb[b][:, 0:n], pt[:, 0:n], gc, rsb[b][:, 0:n], ALU.mult, ALU.mult,
                      [pk, "gcol", f"rsb{b}"], [f"qnb{b}"])
                wt, wk = P.bank("sw", [6, 7])
                P.mm(wt[:, 0:n], permT[:], qnb[b][:, 0:n], True, True, ["permT", f"qnb{b}"], [wk])
                P.tt("dve", t1b[b][:, 0:n], qnb[b][:, 0:n], cosT[:, t0:t0 + n], ALU.mult, [f"qnb{b}", "cosT"], [f"t1b{b}"])
                P.tt("dve", t2b[b][:, 0:n], wt[:, 0:n], sinT[:, t0:t0 + n], ALU.mult, [wk, "sinT"], [f"t2b{b}"])
                P.tt("dve", stg[sq_][:, 0:n], t1b[b][:, 0:n], t2b[b][:, 0:n], ALU.add, [f"t1b{b}", f"t2b{b}"], [f"stg{sq_}"])
                dst = QT_d if isq else KT_d
                P.dma("sp", dst[head, :, t0:t0 + n], stg[sq_][:, 0:n], reads=[f"stg{sq_}"])
    P.emit()
    return nc


def build_b_odd():
    nc = bass.Bass("TRN2", target_bir_lowering=False)
    NKT = SEQ // 128
    xT_d = _din(nc, "xT", [128, KT, NTOK])
    modT_d = _din(nc, "modT", [128, 2, 3, KT])
    QT_d = _din(nc, "QT", [16, 128, NTOK], BF16)
    GT_d = _din(nc, "GT", [16, 128, NTOK], BF16)
    KA_d = _din(nc, "KTall", [4, 128, SEQ], BF16)
    VA_d = _din(nc, "Vall", [4, 128, NKT, 128], BF16)
    KC_d = _din(nc, "KTc", [4, 128, NCTX], BF16)
    VC_d = _din(nc, "Vc", [4, 128, 2, 128], BF16)
    wo_d = _din(nc, "w_out", [D, D])
    lnT_d = _din(nc, "lnT", [128, 2, KT])
    xo_d = _dout(nc, "xo", [128, KT, NTOK])
    P = Prog(nc)
    sb0 = lambda name, shape, dt: nc.alloc_sbuf_tensor("s_" + name, shape, dt)
    xT, modT = load_x_and_mod(P, nc, xT_d, modT_d)
    zT = sb0("zT", [128, KT, NTOK], BF16)
    with ExitStack() as st:
        sb = lambda name, shape, dt: st.enter_context(nc.sbuf_tensor("s_" + name, shape, dt))
        Ksb = sb("Ksb", [128, SEQ + NCTX], BF16)
        Vsb = sb("Vsb", [128, NKT + 2, 128], BF16)
        Qsb = [sb(f"Qsb{i}", [128, NTOK], BF16) for i in range(2)]
        Gsb = [sb(f"Gsb{i}", [128, NTOK], BF16) for i in range(2)]
        PT = [sb(f"PT{i}", [128, 512], BF16) for i in range(3)]
        rec = [sb(f"rec{i}", [128, 512], F32) for i in range(2)]
        z1 = [sb(f"z1{i}", [128, 512], F32) for i in range(2)]
        onesb = sb("onesb", [128, 128], BF16)
        P.memset("pool", onesb[:], 1.0, ["onesb"])
        pi = 0
        fi = 0
        for kv in range(4):
            for kk in range(4):
                P.dma("sp", Ksb[:, 2048 * kk:2048 * (kk + 1)], KA_d[kv, :, 2048 * kk:2048 * (kk + 1)], writes=[f"K{kk}"])
                P.dma("sp", Vsb[:, 16 * kk:16 * (kk + 1), :], VA_d[kv, :, 16 * kk:16 * (kk + 1), :], writes=[f"V{kk}"])
            P.dma("sp", Ksb[:, SEQ:SEQ + NCTX], KC_d[kv], writes=["K4"])
            P.dma("sp", Vsb[:, NKT:NKT + 2, :], VC_d[kv], writes=["V4"])
            for hh in range(4):
                h = kv * 4 + hh
                qb = h % 2
                P.dma("sp", Qsb[qb][:], QT_d[h], writes=[f"Q{qb}"])
                P.dma("sp", Gsb[qb][:], GT_d[h], writes=[f"G{qb}"])
                for (t0, n) in TB:
                    kts = list(range(NKT + 2)) if t0 < NLAT else [NKT, NKT + 1]
                    ot, ok = P.bank("O", [4, 5])
                    dt_, dk = P.bank("Dn", [6, 7])
                    sl = {}

                    def qk(j_):
                        kt_ = kts[j_]
                        s_t_, s_k_ = P.bank("S", [0, 1, 2, 3])
                        P.mm(s_t_[:, 0:n], Ksb[:, kt_ * 128:(kt_ + 1) * 128], Qsb[qb][:, t0:t0 + n], True, True,
                             [f"K{min(kt_ // 16, 4)}", f"Q{qb}"], [s_k_])
                        sl[j_] = (s_t_, s_k_)
                    qk(0)
                    if len(kts) > 1:
                        qk(1)
                    for j, kt in enumerate(kts):
                        if j + 2 < len(kts):
                            qk(j + 2)
                        s_t, s_k = sl.pop(j)
                        p = pi % 3
                        pi += 1
                        P.act(PT[p][:, 0:n], s_t[:, 0:n], AF.Exp, [s_k], [f"PT{p}"])
                        P.mm(ot[:, 0:n], Vsb[:, kt, :], PT[p][:, 0:n], j == 0, j == len(kts) - 1,
                             [f"V{min(kt // 16, 4)}", f"PT{p}"], [ok])
                        P.mm(dt_[:, 0:n], onesb[:], PT[p][:, 0:n], j == 0, j == len(kts) - 1,
                             ["onesb", f"PT{p}"], [dk])
                    f = fi % 2
                    fi += 1
                    P.op("dve", (lambda o_, i_: (lambda e: e.reciprocal(o_, i_)))(rec[f][:, 0:n], dt_[:, 0:n]),
                         [dk], [f"rec{f}"])
                    P.tt("dve", z1[f][:, 0:n], ot[:, 0:n], rec[f][:, 0:n], ALU.mult, [ok, f"rec{f}"], [f"z1{f}"])
                    P.tt("pool", zT[:, h, t0:t0 + n], z1[f][:, 0:n], Gsb[qb][:, t0:t0 + n], ALU.mult,
                         [f"z1{f}", f"G{qb}"], [f"zT{h}"])
        P.barrier()
    with ExitStack() as st:
        outproj_ln(P, nc, st, zT, xT, modT, wo_d, lnT_d, xo_d)
    P.emit()
    return nc


_PROGS = {}
BF = ml_dtypes.bfloat16


def _run(name, builder, in_maps):
    if name not in _PROGS:
        _PROGS[name] = builder()
    res = run_bass_kernel_spmd(_PROGS[name], in_maps, core_ids=list(range(NCORE)))
    return res.results


def to_fm(X):
    T = X.shape[0]
    return np.ascontiguousarray(X.reshape(T, KT, 128).transpose(2, 1, 0))


def from_fm(xT):
    T = xT.shape[2]
    return np.ascontiguousarray(xT.transpose(2, 1, 0).reshape(T, D))


def host_mod(c, c_ctx, ada_w, ada_b):
    cv = np.stack([c.reshape(D), c_ctx.reshape(D)], 1)
    cT = np.ascontiguousarray(cv.reshape(KT, 128, 2).transpose(1, 0, 2))
    in_maps = []
    for j in range(NCORE):
        in_maps.append({"cT": cT,
                        "aw": np.ascontiguousarray(ada_w[:, :, 768 * j:768 * (j + 1)]),
                        "ab": np.ascontiguousarray(ada_b[:, 768 * j:768 * (j + 1)])})
    res = _run("mod", build_mod, in_maps)
    mod = np.concatenate([r["mod"] for r in res], axis=2)
    return mod


def mod_to_T(mod_l):
    return np.ascontiguousarray(mod_l.reshape(2, 3, KT, 128).transpose(3, 0, 1, 2))


def ln_to_T(g, b):
    return np.ascontiguousarray(np.stack([g, b], 0).reshape(2, KT, 128).transpose(2, 0, 1))


def rope_tables():
    t = np.arange(SEQ)
    row = (t // 64).astype(np.float32)
    col = (t % 64).astype(np.float32)
    inv = (np.float32(10000.0) ** (-np.arange(32, dtype=np.float32) / np.float32(32))).astype(np.float32)
    ang = np.concatenate([row[:, None] * inv, col[:, None] * inv], -1).astype(np.float32)
    cos = np.cos(ang).astype(np.float32)
    sin = np.sin(ang).astype(np.float32)
    cosT = np.concatenate([cos, cos], 1).T
    sinT = np.concatenate([-sin, sin], 1).T
    outs = []
    for j in range(NCORE):
        c = np.ones((128, NTOK), np.float32)
        s = np.zeros((128, NTOK), np.float32)
        c[:, :NLAT] = cosT[:, NLAT * j:NLAT * (j + 1)]
        s[:, :NLAT] = sinT[:, NLAT * j:NLAT * (j + 1)]
        outs.append((np.ascontiguousarray(c), np.ascontiguousarray(s)))
    return outs


def run_odd_layer(xTs, modT, w_in, w_out, qg, kg, lnT):
    tabs = rope_tables()
    gcol = np.ascontiguousarray(np.stack([qg, kg], 1).astype(np.float32))
    permT = np.zeros((128, 128), np.float32)
    for m in range(128):
        permT[(m + 64) % 128, m] = 1.0
    in_maps = [{"xT": xTs[j], "modT": modT, "w_in": w_in, "cosT": tabs[j][0], "sinT": tabs[j][1],
                "gcol": gcol, "permT": permT} for j in range(NCORE)]
    ra = _run("a_odd", build_a_odd, in_maps)
    KTall = np.ascontiguousarray(np.concatenate([r["KTo"][:, :, :NLAT] for r in ra], axis=2))
    KTc = np.ascontiguousarray(ra[0]["KTo"][:, :, NLAT:])
    Vlat = np.concatenate([r["V"][:NLAT] for r in ra], axis=0)
    Vall = np.ascontiguousarray(Vlat.reshape(SEQ // 128, 128, 4, 128).transpose(2, 1, 0, 3))
    Vc = np.ascontiguousarray(ra[0]["V"][NLAT:].reshape(2, 128, 4, 128).transpose(2, 1, 0, 3))
    in_maps = [{"xT": xTs[j], "modT": modT, "QT": ra[j]["QT"], "GT": ra[j]["GT"], "KTall": KTall, "Vall": Vall,
                "KTc": KTc, "Vc": Vc, "w_out": w_out, "lnT": lnT} for j in range(NCORE)]
    rb = _run("b_odd", build_b_odd, in_maps)
    return [r["xo"] for r in rb], ra


MAGIC = 12582912.0
SIN_SCALE = TWO_PI - 1e-6


def rev_ap(t, pstep, off, n):
    return bass.AP(t, off + n - 1, [[pstep, 128], [-1, n]])


class S5:
    pass


def s5_prep(P, nc, sb, s5p_d, iota_d, need_F):
    S = S5()
    prm = sb("s5prm", [128, 3, 64], F32)
    P.dma("sp", prm[:], s5p_d, writes=["s5prm"])
    S.iota = sb("s5iota", [128, 1024], F32)
    P.dma("sp", S.iota[:], iota_d, writes=["s5iota"])
    dtc = sb("s5dt", [128, 64], F32)
    lnr = sb("s5lnr", [128, 64], F32)
    S.r = sb("s5r", [128, 64], F32)
    S.th2 = sb("s5th2", [128, 64], F32)
    P.act(dtc[:], prm[:, 2, :], AF.Exp, ["s5prm"], ["s5dt"])
    P.tt("dve", lnr[:], prm[:, 0, :], dtc[:], ALU.mult, ["s5prm", "s5dt"], ["s5lnr"])
    P.act(S.r[:], lnr[:], AF.Exp, ["s5lnr"], ["s5r"])
    P.tt("dve", S.th2[:], prm[:, 1, :], dtc[:], ALU.mult, ["s5prm", "s5dt"], ["s5th2"])
    P.ts("dve", S.th2[:], S.th2[:], 1.0 / TWO_PI, None, ALU.mult, None, ["s5th2"], ["s5th2"])
    ta = sb("s5ta", [128, 64], F32)
    tk = sb("s5tk", [128, 64], F32)
    S.offc = sb("s5offc", [128, 2], F32)
    P.memset("pool", S.offc[:, 0:1], 0.0, ["s5offc"])
    P.memset("pool", S.offc[:, 1:2], 0.25, ["s5offc"])

    def sincos(y_ap, ykey, mult, name):
        c = sb(name + "c", [128, 64], F32)
        s = sb(name + "s", [128, 64], F32)
        for off, dst, dk in ((0.0, s, name + "s"), (0.25, c, name + "c")):
            P.ts("dve", ta[:], y_ap, float(mult), off, ALU.mult, ALU.add, [ykey], ["s5ta"])
            P.ts("dve", tk[:], ta[:], MAGIC, MAGIC, ALU.add, ALU.subtract, ["s5ta"], ["s5tk"])
            P.tt("dve", ta[:], ta[:], tk[:], ALU.subtract, ["s5ta", "s5tk"], ["s5ta"])
            P.act(dst[:], ta[:], AF.Sin, ["s5ta"], [dk], scale=SIN_SCALE)
        return c, s
    S.c1, S.s1 = sincos(S.th2[:], "s5th2", 1.0, "s5o")
    if need_F:
        a_re = prm[:, 0, :]
        a_im = prm[:, 1, :]
        lbr = sb("s5lbr", [128, 64], F32)
        lbi = sb("s5lbi", [128, 64], F32)
        den = sb("s5den", [128, 64], F32)
        t1 = sb("s5t1", [128, 64], F32)
        S.Fre = sb("s5Fre", [128, 64], F32)
        S.Fim = sb("s5Fim", [128, 64], F32)
        S.nFre = sb("s5nFre", [128, 64], F32)
        P.tt("dve", lbr[:], S.r[:], S.c1[:], ALU.mult, ["s5r", "s5oc"], ["s5lbr"])
        P.ts("dve", lbr[:], lbr[:], -1.0, None, ALU.add, None, ["s5lbr"], ["s5lbr"])
        P.tt("dve", lbi[:], S.r[:], S.s1[:], ALU.mult, ["s5r", "s5os"], ["s5lbi"])
        P.tt("dve", den[:], a_re, a_re, ALU.mult, ["s5prm"], ["s5den"])
        P.tt("dve", t1[:], a_im, a_im, ALU.mult, ["s5prm"], ["s5t1"])
        P.tt("dve", den[:], den[:], t1[:], ALU.add, ["s5den", "s5t1"], ["s5den"])
        P.op("dve", lambda e: e.reciprocal(den[:], den[:]), ["s5den"], ["s5den"])
        P.tt("dve", S.Fre[:], lbr[:], a_re, ALU.mult, ["s5lbr", "s5prm"], ["s5Fre"])
        P.tt("dve", t1[:], lbi[:], a_im, ALU.mult, ["s5lbi", "s5prm"], ["s5t1"])
        P.tt("dve", S.Fre[:], S.Fre[:], t1[:], ALU.add, ["s5Fre", "s5t1"], ["s5Fre"])
        P.tt("dve", S.Fre[:], S.Fre[:], den[:], ALU.mult, ["s5Fre", "s5den"], ["s5Fre"])
        P.tt("dve", S.Fim[:], lbi[:], a_re, ALU.mult, ["s5lbi", "s5prm"], ["s5Fim"])
        P.tt("dve", t1[:], lbr[:], a_im, ALU.mult, ["s5lbr", "s5prm"], ["s5t1"])
        P.tt("dve", S.Fim[:], S.Fim[:], t1[:], ALU.subtract, ["s5Fim", "s5t1"], ["s5Fim"])
        P.tt("dve", S.Fim[:], S.Fim[:], den[:], ALU.mult, ["s5Fim", "s5den"], ["s5Fim"])
        P.ts("dve", S.nFre[:], S.Fre[:], -1.0, None, ALU.mult, None, ["s5Fre"], ["s5nFre"])
        rT = sb("s5rT", [128, 64], F32)
        P.act(rT[:], lnr[:], AF.Exp, ["s5lnr"], ["s5rT"], scale=float(NLAT))
        cT_, sT_ = sincos(S.th2[:], "s5th2", float(NLAT), "s5T")
        S.Are = sb("s5Are", [128, 64], F32)
        S.Aim = sb("s5Aim", [128, 64], F32)
        P.tt("dve", S.Are[:], rT[:], cT_[:], ALU.mult, ["s5rT", "s5Tc"], ["s5Are"])
        P.tt("dve", S.Aim[:], rT[:], sT_[:], ALU.mult, ["s5rT", "s5Ts"], ["s5Aim"])
    return S


def s5_alloc_work(S, sb):
    S.tkA = [sb(f"s5tkA{i}", [128, 1024], F32) for i in range(2)]
    S.tkB = [sb(f"s5tkB{i}", [128, 1024], F32) for i in range(2)]
    S.csb = [sb(f"s5cs{i}", [128, 1024], F32) for i in range(2)]
    S.snb = [sb(f"s5sn{i}", [128, 1024], F32) for i in range(2)]
    S.m = [sb(f"s5m{i}", [128, 512], F32) for i in range(4)]
    S.gin = [sb(f"s5gin{i}", [128, 1024], F32) for i in range(2)]
    S.g = [sb(f"s5g{i}", [128, 1024], F32) for i in range(2)]
    S.ecol = sb("s5ecol", [128, 4], F32)
    S.par = 0
    S.cs = S.csb[0]
    S.sn = S.snb[0]
    S.csk = "s5cs0"
    S.snk = "s5sn0"


def s5_use(S, par):
    S.cs = S.csb[par]
    S.sn = S.snb[par]
    S.csk = f"s5cs{par}"
    S.snk = f"s5sn{par}"


def s5_tables(P, S, kp, T, par):
    th = S.th2[:, kp:kp + 1]
    for off, dst, dk, tk, tkk in ((0.0, S.snb[par], f"s5sn{par}", S.tkA[par], f"s5tkA{par}"),
                                  (0.25, S.csb[par], f"s5cs{par}", S.tkB[par], f"s5tkB{par}")):
        P.act(dst[:, 0:T], S.iota[:, 0:T], AF.Identity, ["s5iota", "s5th2", "s5offc"], [dk], scale=th,
              bias=S.offc[:, 0:1] if off == 0.0 else S.offc[:, 1:2])
        P.ts("dve", tk[:, 0:T], dst[:, 0:T], MAGIC, MAGIC, ALU.add, ALU.subtract, [dk], [tkk])
        P.tt("dve", dst[:, 0:T], dst[:, 0:T], tk[:, 0:T], ALU.subtract, [dk, tkk], [dk])
        P.act(dst[:, 0:T], dst[:, 0:T], AF.Sin, [dk], [dk], scale=SIN_SCALE)


def s5_drive_scan(P, S, kp, T, Bre_ap, Bim_ap, bkeys, u_ap_fn, ukeys, init):
    blocks = [(0, 512), (512, 512)] if T == 1024 else [(0, T)]
    for (t0, n) in blocks:
        xr, xrk = P.bank("s5x", [4, 5, 6, 7])
        xi, xik = P.bank("s5x", [4, 5, 6, 7])
        P.mm(xr[:, 0:n], Bre_ap, u_ap_fn(t0, n), True, True, bkeys + ukeys, [xrk])
        P.mm(xi[:, 0:n], Bim_ap, u_ap_fn(t0, n), True, True, bkeys + ukeys, [xik])
        m = S.m
        P.tt("dve", m[0][:, 0:n], xr[:, 0:n], S.cs[:, t0:t0 + n], ALU.mult, [xrk, S.csk], ["s5m0"])
        P.tt("dve", m[1][:, 0:n], xi[:, 0:n], S.sn[:, t0:t0 + n], ALU.mult, [xik, S.snk], ["s5m1"])
        P.tt("dve", m[2][:, 0:n], xi[:, 0:n], S.cs[:, t0:t0 + n], ALU.mult, [xik, S.csk], ["s5m2"])
        P.tt("dve", m[3][:, 0:n], xr[:, 0:n], S.sn[:, t0:t0 + n], ALU.mult, [xrk, S.snk], ["s5m3"])
        P.tt("pool", S.gin[0][:, t0:t0 + n], m[0][:, 0:n], m[1][:, 0:n], ALU.add, ["s5m0", "s5m1"], ["s5gin0"])
        P.tt("pool", S.gin[1][:, t0:t0 + n], m[2][:, 0:n], m[3][:, 0:n], ALU.subtract, ["s5m2", "s5m3"], ["s5gin1"])
    rb = bass.AP(S.r, kp, [[64, 128], [0, T]])
    for c in range(2):
        if init is None:
            ini, ik = 0.0, []
        else:
            ini, ik = init[c], init[2]
        P.op("dve", (lambda o_, d1_, ini_: (lambda e: e.tensor_tensor_scan(o_, rb, d1_, ini_, ALU.mult, ALU.add)))(
            S.g[c][:, 0:T], S.gin[c][:, 0:T], ini), ["s5r", f"s5gin{c}"] + ik, [f"s5g{c}"])


def s5_end_state(P, S, T, ere_ap, eim_ap, ekey):
    e = S.ecol
    gr = S.g[0][:, T - 1:T]
    gi = S.g[1][:, T - 1:T]
    c = S.cs[:, T - 1:T]
    s = S.sn[:, T - 1:T]
    P.tt("dve", e[:, 0:1], gr, c, ALU.mult, ["s5g0", S.csk], ["s5ecol"])
    P.tt("dve", e[:, 1:2], gi, s, ALU.mult, ["s5g1", S.snk], ["s5ecol"])
    P.tt("dve", e[:, 2:3], gr, s, ALU.mult, ["s5g0", S.snk], ["s5ecol"])
    P.tt("dve", e[:, 3:4], gi, c, ALU.mult, ["s5g1", S.csk], ["s5ecol"])
    P.tt("dve", ere_ap, e[:, 0:1], e[:, 1:2], ALU.subtract, ["s5ecol"], [ekey])
    P.tt("dve", eim_ap, e[:, 2:3], e[:, 3:4], ALU.add, ["s5ecol"], [ekey])


def build_a_even():
    nc = bass.Bass("TRN2", target_bir_lowering=False)
    xT_d = _din(nc, "xT", [128, KT, NTOK])
    modT_d = _din(nc, "modT", [128, 2, 3, KT])
    w_d = _din(nc, "w_in", [D, 6144])
    s5p_d = _din(nc, "s5p", [128, 3, 64])
    iota_d = _din(nc, "iota", [128, 1024])
    Bre_d = _din(nc, "Bre", [8, 128, 8, 128])
    Bim_d = _din(nc, "Bim", [8, 128, 8, 128])
    QT_d = _dout(nc, "QT", [8, 128, NTOK], BF16)
    KT_d = _dout(nc, "KTo", [8, 128, NTOK], BF16)
    V_d = _dout(nc, "V", [NTOK, 1024], BF16)
    GA_d = _dout(nc, "GAT", [8, 128, NTOK], BF16)
    UT_d = _dout(nc, "UT", [8, 128, NTOK], F32)
    GB_d = _dout(nc, "GBT", [8, 128, NTOK], BF16)
    E_d = _dout(nc, "E", [128, 2, 64], F32)
    P = Prog(nc)
    sb0 = lambda name, shape, dt: nc.alloc_sbuf_tensor("s_" + name, shape, dt)
    uTb = sb0("uTb", [128, 8, NLAT], BF16)
    uTr = sb0("uTr", [128, 8, NLAT], BF16)
    with ExitStack() as st:
        sb = lambda name, shape, dt: st.enter_context(nc.sbuf_tensor("s_" + name, shape, dt))
        xT = sb("xT", [128, KT, NTOK], F32)
        modT = sb("modT", [128, 2, 3, KT], F32)
        P.dma("sp", modT[:], modT_d, writes=["modT"])
        for kk in range(4):
            P.dma("sp", xT[:, 4 * kk:4 * kk + 4, :], xT_d[:, 4 * kk:4 * kk + 4, :], writes=[f"xT{kk}"])
        sc1 = sb("sc1", [128, 2, KT], F32)
        P.ts("dve", sc1[:], modT[:, :, 1, :], 1.0, None, ALU.add, None, ["modT"], ["sc1"])
        hT = sb("hT", [128, KT, NTOK], BF16)
        for kt in range(KT):
            for s, (t0, n) in ((0, (0, NLAT)), (1, (NLAT, NCTX))):
                P.act(hT[:, kt, t0:t0 + n], xT[:, kt, t0:t0 + n], AF.Identity,
                      [f"xT{kt // 4}", "sc1", "modT"], [f"hT{kt}"],
                      scale=sc1[:, s, kt:kt + 1], bias=modT[:, s, 0, kt:kt + 1])
        bufs = [sb(f"wi{i}", [128, KT, 512], BF16) for i in range(2)]
        issued = {}

        def blk(i):
            if i not in issued:
                q = i % 2
                for kk in range(4):
                    P.dma("pool", bufs[q][:, 4 * kk:4 * kk + 4, :],
                          w_d[512 * kk:512 * (kk + 1), 512 * i:512 * (i + 1)].rearrange("(kt p) c -> p kt c", p=128),
                          writes=[f"wi{q}_{kk}"])
                issued[i] = True
            return bufs[i % 2], f"wi{i % 2}"
        stg = [sb(f"stg{i}", [128, 512], BF16) for i in range(4)]
        stf = [sb(f"stf{i}", [128, 512], F32) for i in range(2)]
        vst = [sb(f"vst{i}", [128, 512], BF16) for i in range(2)]
        si = 0
        fi = 0
        for cb in range(12):
            wb, wkey = blk(cb)
            if cb + 1 < 12:
                blk(cb + 1)
            if cb in (4, 5):
                for tt in range(NTOK // 128):
                    pt, pk = P.bank("main", [0, 1, 2, 3])
                    for kt in range(KT):
                        P.mm(pt[:, :], hT[:, kt, tt * 128:(tt + 1) * 128], wb[:, kt, :], kt == 0, kt == KT - 1,
                             [f"{wkey}_{kt // 4}", f"hT{kt}"], [pk])
                    q = tt % 2
                    P.act(vst[q][:], pt[:, :], AF.Identity, [pk], [f"vst{q}"])
                    P.dma("sp", V_d[tt * 128:(tt + 1) * 128, 512 * (cb - 4):512 * (cb - 3)], vst[q][:], reads=[f"vst{q}"])
                continue
            for c in range(4):
                idx = (cb % 2) * 4 + c
                for (t0, n) in TB:
                    pt, pk = P.bank("main", [0, 1, 2, 3])
                    lin_fm(P, wb, wkey, c, hT, "hT", t0, n, pt, pk)
                    if cb in (8, 9):
                        f = fi % 2
                        fi += 1
                        P.act(stf[f][:, 0:n], pt[:, 0:n], AF.Identity, [pk], [f"stf{f}"])
                        P.dma("sp", UT_d[idx, :, t0:t0 + n], stf[f][:, 0:n], reads=[f"stf{f}"])
                        if t0 < NLAT:
                            P.copy("dve", uTb[:, idx, t0:t0 + n], stf[f][:, 0:n], [f"stf{f}"], [f"uTb{idx}"])
                            P.copy("dve", uTr[:, idx, NLAT - t0 - n:NLAT - t0],
                                   rev_ap(stf[f], 512, 0, n), [f"stf{f}"], [f"uTr{idx}"])
                        continue
                    sq_ = si % 4
                    si += 1
                    if cb in (0, 1):
                        P.act(stg[sq_][:, 0:n], pt[:, 0:n], AF.Identity, [pk], [f"stg{sq_}"], scale=128.0 ** -0.5)
                        dst = QT_d
                    elif cb in (2, 3):
                        P.act(stg[sq_][:, 0:n], pt[:, 0:n], AF.Identity, [pk], [f"stg{sq_}"])
                        dst = KT_d
                    else:
                        P.act(stg[sq_][:, 0:n], pt[:, 0:n], AF.Silu, [pk], [f"stg{sq_}"])
                        dst = GA_d if cb in (6, 7) else GB_d
                    P.dma("sp", dst[idx, :, t0:t0 + n], stg[sq_][:, 0:n], reads=[f"stg{sq_}"])
        P.barrier()
    with ExitStack() as st:
        sb = lambda name, shape, dt: st.enter_context(nc.sbuf_tensor("s_" + name, shape, dt))
        S = s5_prep(P, nc, sb, s5p_d, iota_d, need_F=False)
        s5_alloc_work(S, sb)
        E = sb("Eout", [128, 2, 64], F32)
        Bb = [[sb(f"Bb{q}{c}", [128, 8, 128], BF16) for c in range(2)] for q in range(2)]
        order = [(ct, k, pp) for ct in range(8) for k in range(2) for pp in range(4)]
        kpof = lambda o: o[1] * 32 + 4 * o[0] + o[2]
        s5_tables(P, S, kpof(order[0]), NLAT, 0)
        itn = 0
        for ct in range(8):
            q = ct % 2
            P.dma("pool", Bb[q][0][:], Bre_d[ct], writes=[f"Bb{q}0"])
            P.dma("pool", Bb[q][1][:], Bim_d[ct], writes=[f"Bb{q}1"])
            for k in range(2):
                for pp in range(4):
                    e_ = k * 4 + pp
                    kp = k * 32 + 4 * ct + pp
                    if itn + 1 < len(order):
                        s5_tables(P, S, kpof(order[itn + 1]), NLAT, (itn + 1) % 2)
                    s5_use(S, itn % 2)
                    itn += 1
                    usrc = uTb if k == 0 else uTr
                    ukey = f"uTb{ct}" if k == 0 else f"uTr{ct}"
                    s5_drive_scan(P, S, kp, NLAT, Bb[q][0][:, e_, :], Bb[q][1][:, e_, :], [f"Bb{q}0", f"Bb{q}1"],
                                  (lambda src, ct_: (lambda t0, n: src[:, ct_, t0:t0 + n]))(usrc, ct), [ukey], None)
                    s5_end_state(P, S, NLAT, E[:, 0, kp:kp + 1], E[:, 1, kp:kp + 1], "Eout")
        P.dma("sp", E_d, E[:], reads=["Eout"])
    P.emit()
    return nc


NSLOT = 14


def build_b_even():
    nc = bass.Bass("TRN2", target_bir_lowering=False)
    xT_d = _din(nc, "xT", [128, KT, NTOK])
    modT_d = _din(nc, "modT", [128, 2, 3, KT])
    QT_d = _din(nc, "QT", [8, 128, NTOK], BF16)
    GA_d = _din(nc, "GAT", [8, 128, NTOK], BF16)
    GB_d = _din(nc, "GBT", [8, 128, NTOK], BF16)
    UT_d = _din(nc, "UT", [8, 128, NTOK], F32)
    Kl_d = _din(nc, "Kloc", [8, 128, NSLOT * 128], BF16)
    Vl_d = _din(nc, "Vloc", [128, NSLOT, 1024], BF16)
    Kc_d = _din(nc, "KTc", [8, 128, NCTX], BF16)
    Vc_d = _din(nc, "Vc", [128, 2, 1024], BF16)
    BM_d = _din(nc, "BM", [8, 128, 56 * 128], F32)
    id_d = _din(nc, "ident", [128, 128], BF16)
    s5p_d = _din(nc, "s5p", [128, 3, 64])
    iota_d = _din(nc, "iota", [128, 1024])
    Bre_d = _din(nc, "Bre", [8, 128, 8, 128])
    Bim_d = _din(nc, "Bim", [8, 128, 8, 128])
    Cre_d = _din(nc, "Cre", [8, 128, 8, 128])
    Cim_d = _din(nc, "Cim", [8, 128, 8, 128])
    dcol_d = _din(nc, "dcol", [128, 8])
    gw_d = _din(nc, "glu_w", [1024, 1024])
    gb_d = _din(nc, "glub", [128, 8])
    EE_d = _din(nc, "EE", [128, 7, 2, 64])
    sel_d = _din(nc, "sel", [128, 8, 64])
    wo_d = _din(nc, "w_out", [D, D])
    lnT_d = _din(nc, "lnT", [128, 2, KT])
    xo_d = _dout(nc, "xo", [128, KT, NTOK])
    P = Prog(nc)
    sb0 = lambda name, shape, dt: nc.alloc_sbuf_tensor("s_" + name, shape, dt)
    zT = sb0("zT", [128, KT, NTOK], BF16)
    modT = sb0("modT", [128, 2, 3, KT], F32)
    P.dma("sp", modT[:], modT_d, writes=["modT"])
    onesb = sb0("onesb", [128, 128], BF16)
    P.memset("pool", onesb[:], 1.0, ["onesb"])
    with ExitStack() as st:
        sb = lambda name, shape, dt: st.enter_context(nc.sbuf_tensor("s_" + name, shape, dt))
        Q = sb("Q", [128, 8, NTOK], BF16)
        GA = sb("GA", [128, 8, NTOK], BF16)
        Kl = sb("Kl", [128, 8, NSLOT * 128], BF16)
        Vl = sb("Vl", [128, NSLOT, 1024], BF16)
        Kc = sb("Kc", [128, 8, NCTX], BF16)
        Vc = sb("Vc", [128, 2, 1024], BF16)
        ident = sb("ident", [128, 128], BF16)
        BMb = [sb(f"BMb{i}", [128, 56 * 128], BF16) for i in range(2)]
        PT = [sb(f"PT{i}", [128, 256], BF16) for i in range(3)]
        rec = [sb(f"rec{i}", [128, 256], F32) for i in range(2)]
        z1 = [sb(f"z1{i}", [128, 256], F32) for i in range(2)]
        for h in range(8):
            P.dma("sp", Q[:, h, :], QT_d[h], writes=[f"Q{h}"])
            P.dma("sp", GA[:, h, :], GA_d[h], writes=[f"GA{h}"])
            P.dma("sp", Kl[:, h, :], Kl_d[h], writes=[f"Kl{h}"])
            P.dma("sp", Kc[:, h, :], Kc_d[h], writes=[f"Kc{h}"])
        for b in range(NSLOT):
            P.dma("sp", Vl[:, b, :], Vl_d[:, b, :], writes=[f"Vl{b}"])
        P.dma("sp", Vc[:], Vc_d, writes=["Vc"])
        P.dma("sp", ident[:], id_d, writes=["ident"])
        pi = 0
        fi = 0
        def load_bm(h_):
            for part in range(7):
                P.dma("pool", BMb[h_ % 2][:, 1024 * part:1024 * (part + 1)], BM_d[h_, :, 1024 * part:1024 * (part + 1)],
                      writes=[f"BMb{h_ % 2}"])
        load_bm(0)
        for h in range(8):
            bq = h % 2
            if h + 1 < 8:
                load_bm(h + 1)
            for m in range(9):
                if m < 8:
                    q0, n = 128 * m, 128
                    tiles = [("l", m + i, m * 7 + i) for i in range(7)] + [("c", 0, 0), ("c", 1, 0)]
                else:
                    q0, n = NLAT, NCTX
                    tiles = [("c", 0, 0), ("c", 1, 0)]
                ot, ok = P.bank("O", [4, 5])
                dt_, dk = P.bank("Dn", [6, 7])
                sl = {}

                def qk(j_):
                    kind_, b_, bmi_ = tiles[j_]
                    s_t_, s_k_ = P.bank("S", [0, 1, 2, 3])
                    if kind_ == "l":
                        P.mm(s_t_[:, 0:n], Kl[:, h, b_ * 128:(b_ + 1) * 128], Q[:, h, q0:q0 + n], True, False,
                             [f"Kl{h}", f"Q{h}"], [s_k_])
                        P.mm(s_t_[:, 0:n], ident[:], BMb[bq][:, bmi_ * 128:(bmi_ + 1) * 128], False, True,
                             ["ident", f"BMb{bq}"], [s_k_])
                    else:
                        P.mm(s_t_[:, 0:n], Kc[:, h, b_ * 128:(b_ + 1) * 128], Q[:, h, q0:q0 + n], True, True,
                             [f"Kc{h}", f"Q{h}"], [s_k_])
                    sl[j_] = (s_t_, s_k_)
                qk(0)
                qk(1)
                for j, (kind, b, bmi) in enumerate(tiles):
                    if j + 2 < len(tiles):
                        qk(j + 2)
                    s_t, s_k = sl.pop(j)
                    if kind == "l":
                        vap, vk = Vl[:, b, h * 128:(h + 1) * 128], f"Vl{b}"
                    else:
                        vap, vk = Vc[:, b, h * 128:(h + 1) * 128], "Vc"
                    p = pi % 3
                    pi += 1
                    P.act(PT[p][:, 0:n], s_t[:, 0:n], AF.Exp, [s_k], [f"PT{p}"])
                    P.mm(ot[:, 0:n], vap, PT[p][:, 0:n], j == 0, j == len(tiles) - 1, [vk, f"PT{p}"], [ok])
                    P.mm(dt_[:, 0:n], onesb[:], PT[p][:, 0:n], j == 0, j == len(tiles) - 1, ["onesb", f"PT{p}"], [dk])
                f = fi % 2
                fi += 1
                P.op("dve", (lambda o_, i_: (lambda e: e.reciprocal(o_, i_)))(rec[f][:, 0:n], dt_[:, 0:n]),
                     [dk], [f"rec{f}"])
                P.tt("dve", z1[f][:, 0:n], ot[:, 0:n], rec[f][:, 0:n], ALU.mult, [ok, f"rec{f}"], [f"z1{f}"])
                P.tt("dve", zT[:, h, q0:q0 + n], z1[f][:, 0:n], GA[:, h, q0:q0 + n], ALU.mult,
                     [f"z1{f}", f"GA{h}"], [f"zT{h}"])
        P.barrier()
    with ExitStack() as st:
        sb_outer = lambda name, shape, dt: st.enter_context(nc.sbuf_tensor("s_" + name, shape, dt))
        yb = sb_outer("yb", [128, 8, NTOK], F32)
        st2 = ExitStack()
        sb = lambda name, shape, dt: st2.enter_context(nc.sbuf_tensor("s_" + name, shape, dt))
        S = s5_prep(P, nc, sb, s5p_d, iota_d, need_F=True)
        s5_alloc_work(S, sb)
        dcol = sb("dcol", [128, 8], F32)
        P.dma("sp", dcol[:], dcol_d, writes=["dcol"])
        Ec = sb("Ec", [128, 2, 64], F32)
        EE = sb("EE", [128, 7, 2, 64], F32)
        sel = sb("sel", [128, 8, 64], F32)
        P.dma("sp", EE[:], EE_d, writes=["EE"])
        P.dma("sp", sel[:], sel_d, writes=["sel"])
        gini = sb("gini", [128, 2, 64], F32)
        uf = [sb(f"uf{i}", [128, NTOK], F32) for i in range(2)]
        ub = [sb(f"ub{i}", [128, NTOK], BF16) for i in range(2)]
        ur = [sb(f"ur{i}", [128, NTOK], BF16) for i in range(2)]
        Bb = [[sb(f"Bb{q}{c}", [128, 8, 128], BF16) for c in range(2)] for q in range(2)]
        Cf1 = [sb(f"Cf0{c}", [128, 8, 128], F32) for c in range(2)]
        Cf = [Cf1, Cf1]
        C2 = [[sb(f"C2{q}{c}", [128, 8, 128], BF16) for c in range(2)] for q in range(2)]
        ctmp = sb("ctmp", [128, 128], F32)
        hb = [sb(f"hb{i}", [128, 1024], BF16) for i in range(2)]
        t1 = sb("ybt1", [128, 512], F32)

        def segment(T, tok0, init_fn, want_end):
            blocks = [(0, 512), (512, 512)] if T == 1024 else [(0, T)]
            order = [(ct, k, pp) for ct in range(8) for k in range(2) for pp in range(4)]
            kpof = lambda o: o[1] * 32 + 4 * o[0] + o[2]
            s5_tables(P, S, kpof(order[0]), T, 0)
            itn = [0]
            for ct in range(8):
                q = ct % 2
                P.dma("sp", uf[q][:], UT_d[ct], writes=[f"uf{q}"])
                P.copy("dve", ub[q][:], uf[q][:], [f"uf{q}"], [f"ub{q}"])
                P.copy("dve", ur[q][:, 0:T], rev_ap(uf[q], NTOK, tok0, T), [f"uf{q}"], [f"ur{q}"])
                P.dma("pool", Bb[q][0][:], Bre_d[ct], writes=[f"Bb{q}0"])
                P.dma("pool", Bb[q][1][:], Bim_d[ct], writes=[f"Bb{q}1"])
                P.dma("sp", Cf[q][0][:], Cre_d[ct], writes=["Cf00"])
                P.dma("sp", Cf[q][1][:], Cim_d[ct], writes=["Cf01"])
                for e_ in range(8):
                    kp = (e_ // 4) * 32 + 4 * ct + (e_ % 4)
                    P.ts("pool", ctmp[:], Cf[q][1][:, e_, :], S.Fim[:, kp:kp + 1], None, ALU.mult, None,
                         ["Cf01", "s5Fim"], ["ctmp"])
                    P.stt("dve", C2[q][0][:, e_, :], Cf[q][0][:, e_, :], S.Fre[:, kp:kp + 1], ctmp[:], ALU.mult, ALU.subtract,
                          ["Cf00", "s5Fre", "ctmp"], [f"C2{q}0"])
                    P.ts("pool", ctmp[:], Cf[q][0][:, e_, :], S.Fim[:, kp:kp + 1], None, ALU.mult, None,
                         ["Cf00", "s5Fim"], ["ctmp"])
                    P.stt("dve", C2[q][1][:, e_, :], Cf[q][1][:, e_, :], S.nFre[:, kp:kp + 1], ctmp[:], ALU.mult, ALU.subtract,
                          ["Cf01", "s5nFre", "ctmp"], [f"C2{q}1"])
                ybanks = {}
                for k in range(2):
                    for bi in range(len(blocks)):
                        ybanks[(k, bi)] = (P.ps[k * 2 + bi], f"ps{k * 2 + bi}")
                for k in range(2):
                    for pp in range(4):
                        e_ = k * 4 + pp
                        kp = k * 32 + 4 * ct + pp
                        if itn[0] + 1 < len(order):
                            s5_tables(P, S, kpof(order[itn[0] + 1]), T, (itn[0] + 1) % 2)
                        s5_use(S, itn[0] % 2)
                        itn[0] += 1
                        if k == 0:
                            ufn = (lambda q_: (lambda t0, n: ub[q_][:, tok0 + t0:tok0 + t0 + n]))(q)
                            ukey = f"ub{q}"
                        else:
                            ufn = (lambda q_: (lambda t0, n: ur[q_][:, t0:t0 + n]))(q)
                            ukey = f"ur{q}"
                        s5_drive_scan(P, S, kp, T, Bb[q][0][:, e_, :], Bb[q][1][:, e_, :], [f"Bb{q}0", f"Bb{q}1"],
                                      ufn, [ukey], init_fn(kp))
                        if want_end:
                            s5_end_state(P, S, T, Ec[:, 0, kp:kp + 1], Ec[:, 1, kp:kp + 1], "Ec")
                        m = S.m
                        for (t0, n) in blocks:
                            P.tt("dve", m[0][:, 0:n], S.g[0][:, t0:t0 + n], S.cs[:, t0:t0 + n], ALU.mult, ["s5g0", S.csk], ["s5m0"])
                            P.tt("dve", m[1][:, 0:n], S.g[1][:, t0:t0 + n], S.sn[:, t0:t0 + n], ALU.mult, ["s5g1", S.snk], ["s5m1"])
                            P.tt("dve", hb[0][:, t0:t0 + n], m[0][:, 0:n], m[1][:, 0:n], ALU.subtract, ["s5m0", "s5m1"], ["hb0"])
                            P.tt("dve", m[2][:, 0:n], S.g[0][:, t0:t0 + n], S.sn[:, t0:t0 + n], ALU.mult, ["s5g0", S.snk], ["s5m2"])
                            P.tt("dve", m[3][:, 0:n], S.g[1][:, t0:t0 + n], S.cs[:, t0:t0 + n], ALU.mult, ["s5g1", S.csk], ["s5m3"])
                            P.tt("dve", hb[1][:, t0:t0 + n], m[2][:, 0:n], m[3][:, 0:n], ALU.add, ["s5m2", "s5m3"], ["hb1"])
                        for bi, (t0, n) in enumerate(blocks):
                            yt, yk = ybanks[(k, bi)]
                            P.mm(yt[:, 0:n], C2[q][0][:, e_, :], hb[0][:, t0:t0 + n], pp == 0, False, [f"C2{q}0", "hb0"], [yk])
                            P.mm(yt[:, 0:n], C2[q][1][:, e_, :], hb[1][:, t0:t0 + n], False, pp == 3, [f"C2{q}1", "hb1"], [yk])
                nb = len(blocks)
                for bi, (t0, n) in enumerate(blocks):
                    yf, yfk = ybanks[(0, bi)]
                    ybk_t, ybk = ybanks[(1, nb - 1 - bi)]
                    P.stt("dve", t1[:, 0:n], uf[q][:, tok0 + t0:tok0 + t0 + n], dcol[:, ct:ct + 1], yf[:, 0:n], ALU.mult, ALU.add,
                          [f"uf{q}", "dcol", yfk], ["ybt1"])
                    P.tt("dve", yb[:, ct, tok0 + t0:tok0 + t0 + n], t1[:, 0:n], rev_ap(ybk_t, 512, 0, n), ALU.add,
                         ["ybt1", ybk], [f"yb{ct}"])

        segment(NCTX, NLAT, lambda kp: None, True)
        Sre = sb("chSre", [128, 64], F32)
        Sim = sb("chSim", [128, 64], F32)
        Hre = sb("chHre", [128, 64], F32)
        Him = sb("chHim", [128, 64], F32)
        ca = sb("cha", [128, 64], F32)
        cb_ = sb("chb", [128, 64], F32)
        cc = sb("chc", [128, 64], F32)
        cd = sb("chd", [128, 64], F32)
        P.copy("dve", Sre[:], Ec[:, 0, :], ["Ec"], ["chSre"])
        P.copy("dve", Sim[:], Ec[:, 1, :], ["Ec"], ["chSim"])
        P.tt("dve", Hre[:], Sre[:], sel[:, 0, :], ALU.mult, ["chSre", "sel"], ["chHre"])
        P.tt("dve", Him[:], Sim[:], sel[:, 0, :], ALU.mult, ["chSim", "sel"], ["chHim"])
        for mstep in range(1, 8):
            P.tt("dve", ca[:], S.Are[:], Sre[:], ALU.mult, ["s5Are", "chSre"], ["cha"])
            P.tt("dve", cb_[:], S.Aim[:], Sim[:], ALU.mult, ["s5Aim", "chSim"], ["chb"])
            P.tt("dve", cc[:], S.Are[:], Sim[:], ALU.mult, ["s5Are", "chSim"], ["chc"])
            P.tt("dve", cd[:], S.Aim[:], Sre[:], ALU.mult, ["s5Aim", "chSre"], ["chd"])
            P.tt("dve", ca[:], ca[:], cb_[:], ALU.subtract, ["cha", "chb"], ["cha"])
            P.tt("dve", cc[:], cc[:], cd[:], ALU.add, ["chc", "chd"], ["chc"])
            P.tt("dve", Sre[:], ca[:], EE[:, mstep - 1, 0, :], ALU.add, ["cha", "EE"], ["chSre"])
            P.tt("dve", Sim[:], cc[:], EE[:, mstep - 1, 1, :], ALU.add, ["chc", "EE"], ["chSim"])
            P.tt("dve", ca[:], Sre[:], sel[:, mstep, :], ALU.mult, ["chSre", "sel"], ["cha"])
            P.tt("dve", Hre[:], Hre[:], ca[:], ALU.add, ["chHre", "cha"], ["chHre"])
            P.tt("dve", cc[:], Sim[:], sel[:, mstep, :], ALU.mult, ["chSim", "sel"], ["chc"])
            P.tt("dve", Him[:], Him[:], cc[:], ALU.add, ["chHim", "chc"], ["chHim"])
        P.tt("dve", ca[:], S.c1[:], Hre[:], ALU.mult, ["s5oc", "chHre"], ["cha"])
        P.tt("dve", cb_[:], S.s1[:], Him[:], ALU.mult, ["s5os", "chHim"], ["chb"])
        P.tt("dve", gini[:, 0, :], ca[:], cb_[:], ALU.subtract, ["cha", "chb"], ["gini"])
        P.tt("dve", cc[:], S.s1[:], Hre[:], ALU.mult, ["s5os", "chHre"], ["chc"])
        P.tt("dve", cd[:], S.c1[:], Him[:], ALU.mult, ["s5oc", "chHim"], ["chd"])
        P.tt("dve", gini[:, 1, :], cc[:], cd[:], ALU.add, ["chc", "chd"], ["gini"])
        segment(NLAT, 0, lambda kp: (gini[:, 0, kp:kp + 1], gini[:, 1, kp:kp + 1], ["gini"]), False)
        P.barrier()
        st2.close()
        sb = sb_outer
        gw = sb("gw", [128, 8, 1024], BF16)
        for kk in range(2):
            P.dma("pool", gw[:, 4 * kk:4 * kk + 4, :],
                  gw_d[512 * kk:512 * (kk + 1), :].rearrange("(kt p) c -> p kt c", p=128), writes=[f"gw{kk}"])
        glub = sb("glub", [128, 8], F32)
        P.dma("sp", glub[:], gb_d, writes=["glub"])
        zbb = sb("zbb", [128, 8, NTOK], BF16)
        g1 = [sb(f"g1{i}", [128, NTOK], F32) for i in range(2)]
        for ct in range(8):
            q = ct % 2
            P.tt("dve", g1[q][:], yb[:, ct, :], yb[:, ct, :], ALU.mult, [f"yb{ct}"], [f"g1{q}"])
            P.ts("dve", g1[q][:], g1[q][:], 0.044715, 1.0, ALU.mult, ALU.add, [f"g1{q}"], [f"g1{q}"])
            P.tt("dve", g1[q][:], g1[q][:], yb[:, ct, :], ALU.mult, [f"g1{q}", f"yb{ct}"], [f"g1{q}"])
            P.act(g1[q][:], g1[q][:], AF.Sigmoid, [f"g1{q}"], [f"g1{q}"], scale=1.5957691216057308)
            P.tt("dve", yb[:, ct, :], yb[:, ct, :], g1[q][:], ALU.mult, [f"yb{ct}", f"g1{q}"], [f"yb{ct}"])
            P.copy("dve", zbb[:, ct, :], yb[:, ct, :], [f"yb{ct}"], [f"zbb{ct}"])
        GBs = [sb(f"GBs{i}", [128, NTOK], BF16) for i in range(2)]
        sg = [sb(f"sg{i}", [128, 512], F32) for i in range(2)]
        gi_ = 0
        for co in range(8):
            q = co % 2
            P.dma("sp", GBs[q][:], GB_d[co], writes=[f"GBs{q}"])
            for (t0, n) in TB:
                pt, pk = P.bank("glu", [4, 5, 6, 7])
                for ci in range(8):
                    P.mm(pt[:, 0:n], gw[:, ci, co * 128:(co + 1) * 128], zbb[:, ci, t0:t0 + n], ci == 0, ci == 7,
                         [f"gw{ci // 4}", f"zbb{ci}"], [pk])
                f = gi_ % 2
                gi_ += 1
                P.act(sg[f][:, 0:n], pt[:, 0:n], AF.Sigmoid, [pk, "glub"], [f"sg{f}"], bias=glub[:, co:co + 1])
                P.tt("dve", sg[f][:, 0:n], sg[f][:, 0:n], yb[:, co, t0:t0 + n], ALU.mult, [f"sg{f}", f"yb{co}"], [f"sg{f}"])
                P.tt("pool", zT[:, 8 + co, t0:t0 + n], sg[f][:, 0:n], GBs[q][:, t0:t0 + n], ALU.mult,
                     [f"sg{f}", f"GBs{q}"], [f"zT{8 + co}"])
        P.barrier()
    with ExitStack() as st:
        sb = lambda name, shape, dt: st.enter_context(nc.sbuf_tensor("s_" + name, shape, dt))
        xT = sb("xT", [128, KT, NTOK], F32)
        for kk in range(4):
            P.dma("sp", xT[:, 4 * kk:4 * kk + 4, :], xT_d[:, 4 * kk:4 * kk + 4, :], writes=[f"xT{kk}"])
        outproj_ln(P, nc, st, zT, xT, modT, wo_d, lnT_d, xo_d)
    P.emit()
    return nc


def even_host_prep(a_re, a_im, log_dt, b_re, b_im, c_re, c_im, d, glu_b):
    def colmajor(a):
        return a.reshape(2, 32, 2, 64).transpose(2, 3, 0, 1).reshape(128, 64)
    dtl = np.broadcast_to(log_dt.reshape(2, 32, 2).transpose(2, 0, 1)[:, None, :, :], (2, 64, 2, 32)).reshape(128, 64)
    s5p = np.ascontiguousarray(np.stack([colmajor(a_re), colmajor(a_im), dtl], 1).astype(np.float32))
    Bre = np.zeros((8, 128, 8, 128), np.float32)
    Bim = np.zeros((8, 128, 8, 128), np.float32)
    Cre = np.zeros((8, 128, 8, 128), np.float32)
    Cim = np.zeros((8, 128, 8, 128), np.float32)
    for ct in range(8):
        for k in range(2):
            for pp in range(4):
                for g2 in range(2):
                    g = 8 * ct + 2 * pp + g2
                    g8 = 2 * pp + g2
                    e = k * 4 + pp
                    Bre[ct, g8 * 16:(g8 + 1) * 16, e, g2 * 64:(g2 + 1) * 64] = b_re[k, g].T
                    Bim[ct, g8 * 16:(g8 + 1) * 16, e, g2 * 64:(g2 + 1) * 64] = b_im[k, g].T
                    Cre[ct, g2 * 64:(g2 + 1) * 64, e, g8 * 16:(g8 + 1) * 16] = c_re[k, g].T
                    Cim[ct, g2 * 64:(g2 + 1) * 64, e, g8 * 16:(g8 + 1) * 16] = c_im[k, g].T
    return dict(s5p=s5p, Bre=Bre, Bim=Bim, Cre=Cre, Cim=Cim,
                dcol=np.ascontiguousarray(d.reshape(8, 128).T), glub=np.ascontiguousarray(glu_b.reshape(8, 128).T),
                iota=np.ascontiguousarray(np.tile(np.arange(1024, dtype=np.float32), (128, 1))))


def build_BM(rpb, j):
    m = np.arange(8)[:, None, None, None]
    i = np.arange(7)[None, :, None, None]
    kr2 = np.arange(2)[None, None, :, None]
    qr2 = np.arange(2)[None, None, None, :]
    r = 16 * j + 2 * m + qr2
    gt = 8 * j - 3 + m + i
    kr = 2 * gt + kr2
    rs = np.clip(r - 4, 0, 120)
    vrow = (gt >= 0) & (gt < 64) & (kr >= rs) & (kr < rs + 8)
    dr = np.clip(kr - r + 7, 0, 14) + 0 * vrow
    kc = np.arange(64)[:, None]
    qc = np.arange(64)[None, :]
    cs = np.clip(qc - 8, 0, 48)
    vcol = (kc >= cs) & (kc < cs + 16)
    dc = np.clip(kc - qc + 15, 0, 30)
    drb = np.broadcast_to(dr.transpose(2, 0, 1, 3)[:, None, :, :, :, None], (2, 64, 8, 7, 2, 64))
    dcb = np.broadcast_to(dc[None, :, None, None, None, :], (2, 64, 8, 7, 2, 64))
    val = (np.broadcast_to(vrow.transpose(2, 0, 1, 3)[:, None, :, :, :, None], (2, 64, 8, 7, 2, 64))
           & np.broadcast_to(vcol[None, :, None, None, None, :], (2, 64, 8, 7, 2, 64)))
    g = rpb[:, drb, dcb]
    out = np.where(val[None], g, np.float32(-30000.0)).astype(np.float32)
    return np.ascontiguousarray(out.reshape(8, 128, 56 * 128))


def run_even_layer(xTs, modT, w_in, w_out, lnT, prep, rpb, glu_w):
    in_maps = [{"xT": xTs[j], "modT": modT, "w_in": w_in, "s5p": prep["s5p"], "iota": prep["iota"],
                "Bre": prep["Bre"], "Bim": prep["Bim"]} for j in range(NCORE)]
    ra = _run("a_even", build_a_even, in_maps)
    KT_all = np.concatenate([r["KTo"][:, :, :NLAT] for r in ra], axis=2)
    V_all = np.concatenate([r["V"][:NLAT] for r in ra], axis=0)
    KTc = np.ascontiguousarray(ra[0]["KTo"][:, :, NLAT:])
    Vc = np.ascontiguousarray(ra[0]["V"][NLAT:].reshape(2, 128, 1024).transpose(1, 0, 2))
    ident = np.eye(128, dtype=np.float32).astype(BF)
    E_all = [r["E"] for r in ra]
    EE = np.zeros((128, 7, 2, 64), np.float32)
    for m in range(1, 8):
        EE[:, m - 1, :, 0:32] = E_all[m - 1][:, :, 0:32]
        EE[:, m - 1, :, 32:64] = E_all[8 - m][:, :, 32:64]
    in_maps = []
    for j in range(NCORE):
        Kloc = np.zeros((8, 128, NSLOT * 128), BF)
        Vloc = np.zeros((128, NSLOT, 1024), BF)
        for b in range(NSLOT):
            gt = 8 * j - 3 + b
            if 0 <= gt < 64:
                Kloc[:, :, b * 128:(b + 1) * 128] = KT_all[:, :, gt * 128:(gt + 1) * 128]
                Vloc[:, b, :] = V_all[gt * 128:(gt + 1) * 128, :]
        sel = np.zeros((128, 8, 64), np.float32)
        sel[:, j, 0:32] = 1.0
        sel[:, 7 - j, 32:64] = 1.0
        in_maps.append({"xT": xTs[j], "modT": modT, "QT": ra[j]["QT"], "GAT": ra[j]["GAT"], "GBT": ra[j]["GBT"],
                        "UT": ra[j]["UT"], "Kloc": Kloc, "Vloc": Vloc, "KTc": KTc, "Vc": Vc, "BM": build_BM(rpb, j),
                        "ident": ident, "s5p": prep["s5p"], "iota": prep["iota"], "Bre": prep["Bre"], "Bim": prep["Bim"],
                        "Cre": prep["Cre"], "Cim": prep["Cim"], "dcol": prep["dcol"], "glu_w": glu_w, "glub": prep["glub"],
                        "EE": EE, "sel": sel, "w_out": w_out, "lnT": lnT})
    rb = _run("b_even", build_b_even, in_maps)
    return [r["xo"] for r in rb], ra


def kernel(x, c, ctx, c_ctx, ada_w, ada_b, ln_g, ln_b, ev_w_in, ev_w_out, na_rpb,
           s5_a_re, s5_a_im, s5_log_dt, s5_b_re, s5_b_im, s5_c_re, s5_c_im, s5_d,
           s5_glu_w, s5_glu_b, od_w_in, od_w_out, q_norm_g, k_norm_g):
    f = lambda a: np.ascontiguousarray(np.asarray(a, dtype=np.float32))
    x, c, ctx, c_ctx, ada_w, ada_b, ln_g, ln_b = map(f, (x, c, ctx, c_ctx, ada_w, ada_b, ln_g, ln_b))
    ev_w_in, ev_w_out, na_rpb, od_w_in, od_w_out, q_norm_g, k_norm_g = map(
        f, (ev_w_in, ev_w_out, na_rpb, od_w_in, od_w_out, q_norm_g, k_norm_g))
    s5_a_re, s5_a_im, s5_log_dt, s5_b_re, s5_b_im, s5_c_re, s5_c_im, s5_d, s5_glu_w, s5_glu_b = map(
        f, (s5_a_re, s5_a_im, s5_log_dt, s5_b_re, s5_b_im, s5_c_re, s5_c_im, s5_d, s5_glu_w, s5_glu_b))
    mod = host_mod(c, c_ctx, ada_w, ada_b)
    X = x[0]
    XC = ctx[0]
    xTs = [to_fm(np.concatenate([X[NLAT * j:NLAT * (j + 1)], XC], 0)) for j in range(NCORE)]
    for l in range(4):
        i = l // 2
        modT = mod_to_T(mod[l])
        lnT = ln_to_T(ln_g[l], ln_b[l])
        if l % 2 == 0:
            prep = even_host_prep(s5_a_re[i], s5_a_im[i], s5_log_dt[i], s5_b_re[i], s5_b_im[i],
                                  s5_c_re[i], s5_c_im[i], s5_d[i], s5_glu_b[i])
            xTs, _ = run_even_layer(xTs, modT, ev_w_in[i], ev_w_out[i], lnT, prep, na_rpb[i], s5_glu_w[i])
        else:
            xTs, _ = run_odd_layer(xTs, modT, od_w_in[i], od_w_out[i], q_norm_g[i], k_norm_g[i], lnT)
    out = np.concatenate([from_fm(xTs[j][:, :, :NLAT]) for j in range(NCORE)], 0)
    return np.ascontiguousarray(out.reshape(1, SEQ, D).astype(np.float32))
```
